# Optimizing a Trainium2 kernel written in Bass

```python
import jax, jax.numpy as jnp
from jax import lax
import numpy as np

D_MODEL = 1024
BATCH = 16
SEQ = 4096
DEPTH = 1

SSM_HEADS = 16
SSM_HEAD_DIM = 64
SSM_WIDTH = SSM_HEADS * SSM_HEAD_DIM
SSM_GROUPS = 2
SSM_STATE = 128
SSM_CONV = 4
SSM_CHUNK = 256
SSM_CONV_WIDTH = SSM_WIDTH + 2 * SSM_GROUPS * SSM_STATE
MLA_HEADS = 8
MLA_Q_RANK = 384
MLA_KV_RANK = 256
MLA_NOPE_DIM = 128
MLA_ROPE_DIM = 64
MLA_V_DIM = 128
MLA_WIDTH = MLA_HEADS * MLA_V_DIM
MIX_WIDTH = SSM_WIDTH + MLA_WIDTH
ATTN_BLOCK = 128
ROPE_THETA = 10000.0
IN_SPLITS = (SSM_WIDTH, SSM_CONV_WIDTH, SSM_HEADS, MLA_Q_RANK, MLA_KV_RANK, MLA_ROPE_DIM)
IN_WIDTH = sum(IN_SPLITS)
PEER_HEADS = 8
PEER_N_KEYS = 128
PEER_N_EXPERTS = PEER_N_KEYS ** 2
PEER_HALF_DIM = 128
PEER_TOPK = 16
PEER_TOKEN_BLOCK = 128
DEEPNORM_ALPHA = (2.0 * DEPTH) ** 0.25
DEEPNORM_BETA = (8.0 * DEPTH) ** -0.25
NORM_EPS = 1e-5

kernel_name = 'hybrid_ssd_mla_peer_block'


def layer_norm(x, g, b):
    xf = x.astype(jnp.float32)
    mu = jnp.mean(xf, -1, keepdims=True)
    var = jnp.mean(jnp.square(xf - mu), -1, keepdims=True)
    return ((xf - mu) * lax.rsqrt(var + NORM_EPS) * g + b).astype(x.dtype)


def rms_norm(x, g):
    xf = x.astype(jnp.float32)
    return (xf * lax.rsqrt(jnp.mean(xf * xf, -1, keepdims=True) + NORM_EPS) * g).astype(x.dtype)


def apply_rope(x, cos, sin):
    x1, x2 = jnp.split(x.astype(jnp.float32), 2, axis=-1)
    return jnp.concatenate([x1 * cos - x2 * sin, x2 * cos + x1 * sin], axis=-1).astype(x.dtype)


def causal_dwconv(u, w, b):
    out = lax.conv_general_dilated(u, w[:, None, :], window_strides=(1,),
                                   padding=[(SSM_CONV - 1, 0)],
                                   dimension_numbers=('NWC', 'WIO', 'NWC'),
                                   feature_group_count=u.shape[-1])
    return out + b


def ssd_chunked(xs, dt, A, Bm, Cm):
    b, L, H, P = xs.shape
    G, N = Bm.shape[2], Bm.shape[3]
    E = H // G
    Q = SSM_CHUNK
    pad = (-L) % Q
    padseq = lambda t: jnp.pad(t, [(0, 0), (0, pad)] + [(0, 0)] * (t.ndim - 2))
    xs, dt, Bm, Cm = padseq(xs), padseq(dt), padseq(Bm), padseq(Cm)
    nc = (L + pad) // Q
    xdt = (xs * dt[..., None]).reshape(b, nc, Q, G, E, P)
    a = jnp.moveaxis((dt * A).reshape(b, nc, Q, G, E), 2, -1)
    a_cum = jnp.cumsum(a, axis=-1)
    Bc = Bm.reshape(b, nc, Q, G, N)
    Cc = Cm.reshape(b, nc, Q, G, N)
    causal = jnp.tril(jnp.ones((Q, Q), dtype=bool))
    seg = a_cum[..., :, None] - a_cum[..., None, :]
    decay = jnp.exp(jnp.where(causal, seg, -jnp.inf))
    cb = jnp.einsum('bclgn,bcsgn->bcgls', Cc, Bc)
    y_diag = jnp.einsum('bcgels,bcsgep->bclgep', cb[:, :, :, None] * decay, xdt)
    decay_states = jnp.exp(a_cum[..., -1:] - a_cum)
    states = jnp.einsum('bclgn,bcgel,bclgep->bcgepn', Bc, decay_states, xdt)
    chunk_decay = jnp.exp(a_cum[..., -1])

    def step(h, inp):
        s_c, d_c = inp
        return h * d_c[..., None, None] + s_c, h

    init = jnp.zeros((b, G, E, P, N), xdt.dtype)
    _, prev = lax.scan(step, init, (jnp.moveaxis(states, 1, 0), jnp.moveaxis(chunk_decay, 1, 0)))
    prev = jnp.moveaxis(prev, 0, 1)
    y_off = jnp.einsum('bclgn,bcgepn,bcgel->bclgep', Cc, prev, jnp.exp(a_cum))
    return (y_diag + y_off).reshape(b, nc * Q, H, P)[:, :L]


def mla_heads(q_lat, kv_lat, k_pe, cos, sin, q_norm_g, w_q_b, kv_norm_g, w_kv_b):
    B, S, _ = q_lat.shape
    q = (rms_norm(q_lat, q_norm_g) @ w_q_b).reshape(B, S, MLA_HEADS, MLA_NOPE_DIM + MLA_ROPE_DIM)
    q_nope = q[..., :MLA_NOPE_DIM]
    q_pe = apply_rope(q[..., MLA_NOPE_DIM:], cos[:, None], sin[:, None])
    kv = (rms_norm(kv_lat, kv_norm_g) @ w_kv_b).reshape(B, S, MLA_HEADS, MLA_NOPE_DIM + MLA_V_DIM)
    k_nope, v = kv[..., :MLA_NOPE_DIM], kv[..., MLA_NOPE_DIM:]
    k_pe = apply_rope(k_pe, cos, sin)
    scale = (MLA_NOPE_DIM + MLA_ROPE_DIM) ** -0.5
    nb = S // ATTN_BLOCK
    qn_blocks = jnp.moveaxis(q_nope.reshape(B, nb, ATTN_BLOCK, MLA_HEADS, MLA_NOPE_DIM), 1, 0)
    qp_blocks = jnp.moveaxis(q_pe.reshape(B, nb, ATTN_BLOCK, MLA_HEADS, MLA_ROPE_DIM), 1, 0)
    key_pos = jnp.arange(S)

    def attend(args):
        qn, qp, blk = args
        s = (jnp.einsum('bqhd,bkhd->bhqk', qn, k_nope).astype(jnp.float32)
             + jnp.einsum('bqhr,bkr->bhqk', qp, k_pe).astype(jnp.float32)) * scale
        q_pos = blk * ATTN_BLOCK + jnp.arange(ATTN_BLOCK)
        s = jnp.where(key_pos[None, :] <= q_pos[:, None], s, -jnp.inf)
        p = jax.nn.softmax(s, axis=-1).astype(v.dtype)
        return jnp.einsum('bhqk,bkhd->bqhd', p, v)

    o = lax.map(attend, (qn_blocks, qp_blocks, jnp.arange(nb)))
    return jnp.moveaxis(o, 0, 1).reshape(B, S, MLA_WIDTH)


def hybrid_mixer(h, cos, sin, w_in, conv_w, conv_b, dt_bias, a_log, d_skip, ssm_norm_g,
                 q_norm_g, w_q_b, kv_norm_g, w_kv_b, w_out):
    B, S, _ = h.shape
    proj = h @ w_in
    z, xbc, dt_raw, q_lat, kv_lat, k_pe = jnp.split(proj, np.cumsum(IN_SPLITS)[:-1].tolist(), axis=-1)
    xbc = jax.nn.silu(causal_dwconv(xbc, conv_w, conv_b))
    xs, Bm, Cm = jnp.split(xbc, [SSM_WIDTH, SSM_WIDTH + SSM_GROUPS * SSM_STATE], axis=-1)
    xs = xs.reshape(B, S, SSM_HEADS, SSM_HEAD_DIM).astype(jnp.float32)
    dt = jax.nn.softplus(dt_raw.astype(jnp.float32) + dt_bias.astype(jnp.float32))
    A = -jnp.exp(a_log.astype(jnp.float32))
    y = ssd_chunked(xs, dt, A,
                    Bm.reshape(B, S, SSM_GROUPS, SSM_STATE).astype(jnp.float32),
                    Cm.reshape(B, S, SSM_GROUPS, SSM_STATE).astype(jnp.float32))
    y = y + d_skip.astype(jnp.float32)[:, None] * xs
    y_ssm = rms_norm(y.reshape(B, S, SSM_WIDTH) * jax.nn.silu(z.astype(jnp.float32)), ssm_norm_g).astype(h.dtype)
    y_mla = mla_heads(q_lat, kv_lat, k_pe, cos, sin, q_norm_g, w_q_b, kv_norm_g, w_kv_b).astype(h.dtype)
    return jnp.concatenate([y_ssm, y_mla], axis=-1) @ w_out


def peer(x, w_query, sub_keys, u_table, v_table):
    B, S, D = x.shape
    T = PEER_TOKEN_BLOCK
    K = PEER_TOPK
    xt = x.reshape(-1, T, D)

    def block(xb):
        q = (xb @ w_query).reshape(T, PEER_HEADS, 2, PEER_HALF_DIM)
        s = jnp.einsum('thkd,hknd->thkn', q, sub_keys).astype(jnp.float32)
        sv, si = lax.top_k(s, K)
        cand_s = (sv[:, :, 0, :, None] + sv[:, :, 1, None, :]).reshape(T, PEER_HEADS, K * K)
        cand_i = (si[:, :, 0, :, None] * PEER_N_KEYS + si[:, :, 1, None, :]).reshape(T, PEER_HEADS, K * K)
        top_s, top_pos = lax.top_k(cand_s, K)
        expert_idx = jnp.take_along_axis(cand_i, top_pos, axis=-1)
        gate = jax.nn.softmax(top_s, axis=-1)
        u = jnp.take(u_table, expert_idx, axis=0)
        act = jax.nn.gelu(jnp.einsum('td,thkd->thk', xb, u).astype(jnp.float32), approximate=False)
        v = jnp.take(v_table, expert_idx, axis=0)
        return jnp.einsum('thk,thkd->td', (gate * act).astype(xb.dtype), v)

    return lax.map(block, xt).reshape(B, S, D)


def setup_inputs(seed: int = 0) -> dict:
    key = jax.random.key(seed)
    ks = jax.random.split(key, 24)
    f32 = jnp.float32
    L = DEPTH
    nrm = lambda k, shape, s: jax.random.normal(k, shape, f32) * s
    gain = lambda k, shape: 1.0 + 0.02 * jax.random.normal(k, shape, f32)
    dt0 = jnp.exp(jax.random.uniform(ks[6], (L, SSM_HEADS), f32, np.float32(np.log(1e-3)), np.float32(np.log(1e-1))))
    dt_bias = dt0 + jnp.log(-jnp.expm1(-dt0))
    return {
        'x': jax.random.normal(ks[0], (BATCH, SEQ, D_MODEL), f32),
        'ln_in_g': gain(ks[1], (D_MODEL,)),
        'ln_in_b': nrm(ks[2], (D_MODEL,), 0.02),
        'w_in': nrm(ks[3], (L, D_MODEL, IN_WIDTH), D_MODEL ** -0.5),
        'conv_w': nrm(ks[4], (L, SSM_CONV, SSM_CONV_WIDTH), SSM_CONV ** -0.5),
        'conv_b': nrm(ks[5], (L, SSM_CONV_WIDTH), 0.02),
        'dt_bias': dt_bias,
        'a_log': jnp.log(jax.random.uniform(ks[7], (L, SSM_HEADS), f32, 1.0, 16.0)),
        'd_skip': gain(ks[8], (L, SSM_HEADS)),
        'ssm_norm_g': gain(ks[9], (L, SSM_WIDTH)),
        'q_norm_g': gain(ks[10], (L, MLA_Q_RANK)),
        'w_q_b': nrm(ks[11], (L, MLA_Q_RANK, MLA_HEADS * (MLA_NOPE_DIM + MLA_ROPE_DIM)), MLA_Q_RANK ** -0.5),
        'kv_norm_g': gain(ks[12], (L, MLA_KV_RANK)),
        'w_kv_b': nrm(ks[13], (L, MLA_KV_RANK, MLA_HEADS * (MLA_NOPE_DIM + MLA_V_DIM)), MLA_KV_RANK ** -0.5),
        'w_out': nrm(ks[14], (L, MIX_WIDTH, D_MODEL), DEEPNORM_BETA * MIX_WIDTH ** -0.5),
        'ln_mix_g': gain(ks[15], (L, D_MODEL)),
        'ln_mix_b': nrm(ks[16], (L, D_MODEL), 0.02),
        'w_query': nrm(ks[17], (L, D_MODEL, PEER_HEADS * 2 * PEER_HALF_DIM), D_MODEL ** -0.5),
        'sub_keys': nrm(ks[18], (L, PEER_HEADS, 2, PEER_N_KEYS, PEER_HALF_DIM), PEER_HALF_DIM ** -0.5),
        'u_table': nrm(ks[19], (L, PEER_N_EXPERTS, D_MODEL), D_MODEL ** -0.5),
        'v_table': nrm(ks[20], (L, PEER_N_EXPERTS, D_MODEL), DEEPNORM_BETA * PEER_HEADS ** -0.5),
        'ln_ffn_g': gain(ks[21], (L, D_MODEL)),
        'ln_ffn_b': nrm(ks[22], (L, D_MODEL), 0.02),
    }


def reference(x, ln_in_g, ln_in_b, w_in, conv_w, conv_b, dt_bias, a_log, d_skip, ssm_norm_g,
              q_norm_g, w_q_b, kv_norm_g, w_kv_b, w_out, ln_mix_g, ln_mix_b,
              w_query, sub_keys, u_table, v_table, ln_ffn_g, ln_ffn_b):
    S = x.shape[1]
    half = MLA_ROPE_DIM // 2
    inv_freq = ROPE_THETA ** (-jnp.arange(half, dtype=jnp.float32) / half)
    ang = jnp.arange(S, dtype=jnp.float32)[:, None] * inv_freq
    cos, sin = jnp.cos(ang), jnp.sin(ang)
    h = layer_norm(x, ln_in_g, ln_in_b)
    for l in range(DEPTH):
        mix = hybrid_mixer(h, cos, sin, w_in[l], conv_w[l], conv_b[l], dt_bias[l], a_log[l], d_skip[l],
                           ssm_norm_g[l], q_norm_g[l], w_q_b[l], kv_norm_g[l], w_kv_b[l], w_out[l])
        h = layer_norm(DEEPNORM_ALPHA * h + mix, ln_mix_g[l], ln_mix_b[l])
        ffn = peer(h, w_query[l], sub_keys[l], u_table[l], v_table[l])
        h = layer_norm(DEEPNORM_ALPHA * h + ffn, ln_ffn_g[l], ln_ffn_b[l])
    return h
```

```python
from contextlib import ExitStack
import numpy as np
import concourse.bass as bass
import concourse.mybir as mybir
from concourse.bass_utils import run_bass_kernel_spmd

F32 = mybir.dt.float32
BF16 = mybir.dt.bfloat16
AF = mybir.ActivationFunctionType
ALU = mybir.AluOpType
AX = mybir.AxisListType

D = 1024
EPS = 1e-5
ALPHA = 2.0 ** 0.25
NEXP = 16384
WRITE_KEYS = ("out", "accum_out", "ap")
NDS = 40
PIPE = True


class Ev:
    __slots__ = ("sem", "val")

    def __init__(self, sem, val):
        self.sem = sem
        self.val = val


class Eng:
    def __init__(self, e, sem, is_pe=False):
        self.name = None
        self.last_idx = None
        self.e = e
        self.sem = sem
        self.cnt = 0
        self.seen = {}
        self.is_pe = is_pe
        self.next_ev = Ev(sem, None)
        self.last_ins = None


class Tracker:
    def __init__(self, nc, stack):
        self.nc = nc
        self.E = {}
        for name, e, pe in (("pe", nc.tensor, True), ("act", nc.scalar, False),
                            ("dve", nc.vector, False), ("pool", nc.gpsimd, False)):
            self.E[name] = Eng(e, stack.enter_context(nc.semaphore("s_" + name)), pe)
        self.E["sp"] = Eng(nc.sync, None)
        for k, v in self.E.items():
            v.name = k
        self.dsem = [stack.enter_context(nc.semaphore("dma%d" % i)) for i in range(NDS)]
        self.bar_sem = stack.enter_context(nc.semaphore("s_bar"))
        self.bar_cnt = 0
        self.duse = [0] * NDS
        self.dnext = 0
        self.W = {}
        self.R = {}
        self.psum_names = set()
        self.log = {k: [] for k in self.E}
        self.semname = {}

    def _keys(self, kw):
        r, w = [], []
        for k, v in kw.items():
            if k in ("_r", "_w"):
                continue
            if hasattr(v, "tensor"):
                name = v.tensor.name
                if k in WRITE_KEYS or name in self.psum_names:
                    w.append(name)
                else:
                    r.append(name)
        r += kw.get("_r", [])
        w += kw.get("_w", [])
        return r, w

    def _flush_pe(self):
        pe = self.E["pe"]
        if pe.last_ins is not None:
            pe.cnt += 1
            pe.last_ins.then_inc(pe.sem, 1)
            self.log["pe"][pe.last_idx][2].append((id(pe.sem), 1))
            pe.next_ev.val = pe.cnt
            pe.next_ev = Ev(pe.sem, None)
            pe.last_ins = None

    def _wait(self, E, rk, wk):
        need = {}

        def add(ev):
            if ev is None:
                return
            if E.is_pe and ev.sem is E.sem:
                return
            if ev.val is None:
                self._flush_pe()
            k = id(ev.sem)
            if k not in need or need[k][1] < ev.val:
                need[k] = (ev.sem, ev.val)

        for k in rk:
            add(self.W.get(k))
        for k in wk:
            add(self.W.get(k))
            for ev in self.R.get(k, {}).values():
                add(ev)
        for k, (sem, val) in need.items():
            if E.seen.get(k, 0) < val:
                E.e.wait_ge(sem, val)
                E.seen[k] = val
                self.log[E.name].append(("w", k, val))

    def _record(self, ev, rk, wk):
        for k in wk:
            self.W[k] = ev
            self.R[k] = {}
        for k in rk:
            if k in wk:
                continue
            self.R.setdefault(k, {})[id(ev.sem)] = ev

    def op(self, eng, fname, **kw):
        E = self.E[eng]
        rk, wk = self._keys(kw)
        self._wait(E, rk, wk)
        args = {k: v for k, v in kw.items() if k not in ("_r", "_w")}
        ins = getattr(E.e, fname)(**args)
        if E.is_pe:
            E.last_ins = ins
            E.last_idx = len(self.log[eng])
            self.log[eng].append(("i", fname, []))
            ev = E.next_ev
        else:
            E.cnt += 1
            ins.then_inc(E.sem, 1)
            self.log[eng].append(("i", fname, [(id(E.sem), 1)]))
            ev = Ev(E.sem, E.cnt)
        self._record(ev, rk, wk)
        return ins

    def dma(self, eng, out, in_, _r=(), _w=(), **kw):
        E = self.E[eng]
        rk = [in_.tensor.name] + list(_r)
        wk = [out.tensor.name] + list(_w)
        self._wait(E, rk, wk)
        s = self.dnext
        self.dnext = (self.dnext + 1) % NDS
        sem = self.dsem[s]
        if self.duse[s] > 0 and E.seen.get(id(sem), 0) < 16 * self.duse[s]:
            E.e.wait_ge(sem, 16 * self.duse[s])
            E.seen[id(sem)] = 16 * self.duse[s]
            self.log[eng].append(("w", id(sem), 16 * self.duse[s]))
        self.duse[s] += 1
        E.e.dma_start(out=out, in_=in_, **kw).then_inc(sem, 16)
        self.log[eng].append(("i", "dma", [(id(sem), 16)]))
        ev = Ev(sem, 16 * self.duse[s])
        self._record(ev, rk, wk)

    def barrier(self):
        self._flush_pe()
        sp = self.E["sp"]
        targets = []
        for name in ("pe", "act", "dve", "pool"):
            E = self.E[name]
            if E.cnt > 0:
                targets.append((E.sem, E.cnt))
        for i in range(NDS):
            if self.duse[i] > 0:
                targets.append((self.dsem[i], 16 * self.duse[i]))
        for sem, val in targets:
            k = id(sem)
            if sp.seen.get(k, 0) < val:
                sp.e.wait_ge(sem, val)
                sp.seen[k] = val
                self.log["sp"].append(("w", k, val))
        self.bar_cnt += 1
        sp.e.sem_inc(self.bar_sem, 1)
        self.log["sp"].append(("i", "sem_inc", [(id(self.bar_sem), 1)]))
        for name in ("pe", "act", "dve", "pool"):
            E = self.E[name]
            E.e.wait_ge(self.bar_sem, self.bar_cnt)
            self.log[name].append(("w", id(self.bar_sem), self.bar_cnt))
            for sem, val in targets:
                E.seen[id(sem)] = max(E.seen.get(id(sem), 0), val)

    def pe_signal(self):
        self._flush_pe()

    def finish(self, eng, keys):
        self._wait(self.E[eng], [], list(keys))
        self.simulate()

    def simulate(self):
        sems = {}
        pc = {k: 0 for k in self.log}
        total = sum(len(v) for v in self.log.values())
        done = 0
        while True:
            progress = False
            for k, lst in self.log.items():
                while pc[k] < len(lst):
                    e = lst[pc[k]]
                    if e[0] == "w":
                        if sems.get(e[1], 0) < e[2]:
                            break
                    else:
                        for sid, amt in e[2]:
                            sems[sid] = sems.get(sid, 0) + amt
                    pc[k] += 1
                    done += 1
                    progress = True
            if done == total:
                return True
            if not progress:
                msg = {k: (pc[k], len(self.log[k]), self.log[k][pc[k]] if pc[k] < len(self.log[k]) else None) for k in self.log}
                raise RuntimeError("semaphore deadlock in host simulation: %r" % (msg,))


def _consts(S):
    ident = np.eye(128, dtype=np.float32)
    triu = np.triu(np.ones((128, 128), np.float32))
    negm = np.where(triu > 0, 0.0, -30000.0).astype(np.float32)
    half = 32
    inv_freq = (10000.0 ** (-np.arange(half, dtype=np.float32) / half)).astype(np.float32)
    ang = (np.arange(S, dtype=np.float32)[:, None] * inv_freq).astype(np.float32)
    cos, sin = np.cos(ang).astype(np.float32), np.sin(ang).astype(np.float32)
    return {
        "c_ident": ident, "c_triu": triu, "c_negm4": np.tile(negm, (1, 4)),
        "c_cos": cos, "c_sin": sin,
        "c_cos2t": np.ascontiguousarray(np.concatenate([cos, cos], 1).T),
        "c_sin2t": np.ascontiguousarray(np.concatenate([sin, sin], 1).T),
    }


def build(NSEQ, S, debug=False, stop_after="all"):
    nc = bass.Bass("TRN2", target_bir_lowering=False)
    NT = S // 128
    NTOK = NSEQ * S
    skind = "ExternalOutput" if debug else "Internal"

    def din(name, shape, dt=F32):
        return nc.dram_tensor(name, list(shape), dt, kind="ExternalInput").ap()

    x = din("x", [NSEQ, S, D])
    ln_in_g, ln_in_b = din("ln_in_g", [D]), din("ln_in_b", [D])
    w_in = din("w_in", [D, 3280])
    conv_w, conv_b = din("conv_w", [4, 1536]), din("conv_b", [1536])
    dt_bias, a_log, d_skip = din("dt_bias", [16]), din("a_log", [16]), din("d_skip", [16])
    ssm_norm_g = din("ssm_norm_g", [D])
    q_norm_g, w_q_b = din("q_norm_g", [384]), din("w_q_b", [384, 1536])
    kv_norm_g, w_kv_b = din("kv_norm_g", [256]), din("w_kv_b", [256, 2048])
    w_out = din("w_out", [2048, D])
    ln_mix_g, ln_mix_b = din("ln_mix_g", [D]), din("ln_mix_b", [D])
    w_query = din("w_query", [D, 2048])
    sub_keys = din("sub_keys", [16, 128, 128])
    u_table, v_table = din("u_table", [NEXP, D]), din("v_table", [NEXP, D])
    ln_ffn_g, ln_ffn_b = din("ln_ffn_g", [D]), din("ln_ffn_b", [D])
    c_ident, c_triu, c_negm4 = din("c_ident", [128, 128]), din("c_triu", [128, 128]), din("c_negm4", [128, 512])
    c_cos, c_sin = din("c_cos", [S, 32]), din("c_sin", [S, 32])
    c_cos2t, c_sin2t = din("c_cos2t", [64, S]), din("c_sin2t", [64, S])

    y_out = nc.dram_tensor("y_out", [NSEQ, S, D], F32, kind="ExternalOutput").ap()
    YS = nc.dram_tensor("s_ys", [NSEQ, 128, 8, S], BF16, kind=skind).ap()
    YM = nc.dram_tensor("s_ym", [NSEQ, 128, 8, S], BF16, kind=skind).ap()
    QLs = nc.dram_tensor("s_ql", [NSEQ, 128, 3, S], BF16, kind=skind).ap()
    KVLs = nc.dram_tensor("s_kvl", [NSEQ, 128, 2, S], BF16, kind=skind).ap()
    KPEs = nc.dram_tensor("s_kpe", [NSEQ, 64, S], BF16, kind=skind).ap()
    UT = nc.dram_tensor("s_ut", [8, 128, NEXP], BF16, kind="Internal").ap()
    VB = nc.dram_tensor("s_vb", [NEXP, D], BF16, kind="Internal").ap()

    with ExitStack() as top:
        T = Tracker(nc, top)

        def sb(name, shape, dt=F32, stack=top):
            return stack.enter_context(nc.sbuf_tensor(name, list(shape), dt))

        PS = []
        for i in range(8):
            t = top.enter_context(nc.psum_tensor("ps%d" % i, [128, 512], F32))
            T.psum_names.add("ps%d" % i)
            PS.append(t)
        pscur = [0]

        def psum():
            t = PS[pscur[0]]
            pscur[0] = (pscur[0] + 1) % 8
            return t

        def psum_from(pool, cur):
            t = PS[pool[cur[0] % len(pool)]]
            cur[0] += 1
            return t

        identf = sb("identf", [128, 128])
        identb = sb("identb", [128, 128], BF16)
        triu = sb("triu", [128, 128])
        ntriu = sb("ntriu", [128, 128])
        onesf = sb("onesf", [128, 128])
        negm4 = sb("negm4", [128, 512])
        T.dma("sp", out=identf[:], in_=c_ident)
        T.dma("sp", out=triu[:], in_=c_triu)
        T.dma("sp", out=negm4[:], in_=c_negm4)
        T.op("dve", "tensor_copy", out=identb[:], in_=identf[:])
        T.op("dve", "tensor_scalar", out=ntriu[:], in0=triu[:], scalar1=-1.0, scalar2=None, op0=ALU.mult)
        T.op("pool", "memset", ap=onesf[:], constant=1.0)

        def bcast_load(name, src, n, stack=top):
            t = sb(name, [128, n], F32, stack)
            T.dma("sp", out=t[:], in_=src.partition_broadcast(128))
            return t

        rr = [0]

        def cast_copy(out, in_):
            e = ("dve", "act", "pool")[rr[0] % 3]
            rr[0] += 1
            if e == "act":
                T.op("act", "copy", out=out, in_=in_)
            else:
                T.op(e, "tensor_copy", out=out, in_=in_)

        def layer_norm(xt, gb, bb, out, tmp, st):
            for h in range(2):
                T.op("dve", "bn_stats", out=st[:, h * 6:(h + 1) * 6], in_=xt[:, h * 512:(h + 1) * 512])
            T.op("dve", "bn_aggr", out=st[:, 12:14], in_=st[:, 0:12])
            T.op("act", "activation", out=st[:, 14:15], in_=st[:, 13:14], func=AF.Sqrt, bias=EPS, scale=1.0)
            T.op("dve", "reciprocal", out=st[:, 15:16], in_=st[:, 14:15])
            T.op("dve", "tensor_scalar", out=tmp[:], in0=xt[:], scalar1=st[:, 12:13], scalar2=st[:, 15:16],
                 op0=ALU.subtract, op1=ALU.mult)
            T.op("pool", "tensor_tensor", out=tmp[:], in0=tmp[:], in1=gb[:], op=ALU.mult)
            T.op("dve", "tensor_tensor", out=out, in0=tmp[:], in1=bb[:], op=ALU.add)

        ln12 = ExitStack()
        g1b = bcast_load("g1b", ln_in_g, D, ln12)
        b1b = bcast_load("b1b", ln_in_b, D, ln12)
        g2b = bcast_load("g2b", ln_mix_g, D, ln12)
        b2b = bcast_load("b2b", ln_mix_b, D, ln12)

        with ExitStack() as pa:
            win = sb("win", [128, 8, 3280], BF16, pa)
            w_in_v = w_in.rearrange("(k p) n -> p k n", p=128)
            with ExitStack() as pws:
                stg = [sb("wstg%d" % i, [128, 8, 410], F32, pws) for i in range(2)]
                for blk in range(8):
                    st_ = stg[blk % 2]
                    T.dma("sp", out=st_[:], in_=w_in_v[:, :, blk * 410:(blk + 1) * 410])
                    cast_copy(win[:, :, blk * 410:(blk + 1) * 410], st_[:])
                T.barrier()
            cw = sb("cw", [128, 12, 4], F32, pa)
            cb = sb("cb", [128, 12], F32, pa)
            for k in range(4):
                T.dma("sp", out=cw[:, :, k:k + 1], in_=conv_w[k].rearrange("(c p o) -> p c o", p=128, o=1),
                      allow_slow_non_contiguous=True)
            T.dma("sp", out=cb[:].unsqueeze(2), in_=conv_b.rearrange("(c p o) -> p c o", p=128, o=1),
                  allow_slow_non_contiguous=True)
            dtb = bcast_load("dtb", dt_bias, 16, pa)
            Ab = bcast_load("Ab", a_log, 16, pa)
            dskb = bcast_load("dskb", d_skip, 16, pa)
            T.op("act", "activation", out=Ab[:], in_=Ab[:], func=AF.Exp)
            T.op("dve", "tensor_scalar", out=Ab[:], in0=Ab[:], scalar1=-1.0, scalar2=None, op0=ALU.mult)
            cosT = sb("cosT", [128, NT, 32], F32, pa)
            sinT = sb("sinT", [128, NT, 32], F32, pa)
            T.dma("sp", out=cosT[:], in_=c_cos.rearrange("(t p) f -> p t f", p=128))
            T.dma("sp", out=sinT[:], in_=c_sin.rearrange("(t p) f -> p t f", p=128))

            xt2 = [sb("xt%d" % i, [128, D], F32, pa) for i in range(3)]
            tmpA = sb("tmpA", [128, D], F32, pa)
            stA = sb("stA", [128, 16], F32, pa)
            hb = sb("hb", [128, D], BF16, pa)
            hT = sb("hT", [128, 8, 128], BF16, pa)
            zs2 = [sb("zs%d" % i, [128, D], F32, pa) for i in range(2)]
            dtv = sb("dtv", [128, 16], F32, pa)
            dtw = sb("dtw", [128, 4, 16], F32, pa)
            dt2 = [sb("dt_t%d" % i, [128, 16], F32, pa) for i in range(2)]
            a2 = [sb("a_t%d" % i, [128, 16], F32, pa) for i in range(2)]
            junk1 = sb("junk1", [128, 640], F32, pa)
            junk2 = sb("junk2", [128, 1024], F32, pa)
            ssq1 = sb("ssq1", [128, 16], F32, pa)
            ssq2 = sb("ssq2", [128, 16], F32, pa)
            latn = sb("latn", [128, 640], BF16, pa)
            kp = sb("kp", [128, 64], F32, pa)
            kpw = sb("kpw", [128, 4, 32], F32, pa)
            kpr = sb("kpr", [128, 64], BF16, pa)
            lat_sb = sb("lat_sb", [128, 768], BF16, pa)
            xraw2 = [sb("xraw%d" % i, [128, 12, 131], F32, pa) for i in range(2)]
            xacc = sb("xacc", [128, 12, 128], F32, pa)
            xtmp = sb("xtmp", [128, 12, 128], F32, pa)
            xact = sb("xact", [128, 12, 128], F32, pa)
            bcb = sb("bcb", [128, 4, 128], BF16, pa)
            xs_tok = sb("xs_tok", [128, D], F32, pa)
            btok = sb("btok", [128, 256], BF16, pa)
            Rm = sb("Rm", [128, 8, 128], F32, pa)
            Abc = sb("Abc", [128, 8, 128], F32, pa)
            expA = sb("expA", [128, 8, 128], F32, pa)
            decT = sb("decT", [128, 8, 128], F32, pa)
            cb_sb = sb("cb_sb", [128, 128], F32, pa)
            MT = sb("MT", [128, 8, 128], BF16, pa)
            CsT = sb("CsT", [128, 8, 128], BF16, pa)
            xdt = sb("xdt", [128, 8, 64], BF16, pa)
            w2 = sb("w2", [128, 8], F32, pa)
            xdtd = sb("xdtd", [128, 8, 64], BF16, pa)
            prevT = sb("prevT", [128, D], F32, pa)
            prevb = sb("prevb", [128, D], BF16, pa)
            ysb = sb("ysb", [128, D], F32, pa)
            ytmp = sb("ytmp", [128, 512], F32, pa)
            ynb = sb("ynb", [128, D], BF16, pa)
            ysT = sb("ysT", [128, 8, 128], BF16, pa)

            tiles = [(seq_, tt_) for seq_ in range(NSEQ) for tt_ in range(NT)]
            P1, P2 = [0, 1, 2, 3], [4, 5, 6, 7]
            c1, c2 = [0], [0]

            def a_load(m):
                sq_, t_ = tiles[m]
                T.dma("sp", out=xt2[m % 3][:], in_=x[sq_, t_ * 128:(t_ + 1) * 128, :])

            def stage1(n):
                seq, tt = tiles[n]
                zs_, dt_, a_, xr = zs2[n % 2], dt2[n % 2], a2[n % 2], xraw2[n % 2]
                ts = slice(tt * 128, (tt + 1) * 128)
                xt = xt2[n % 3]
                if n == 0:
                    a_load(0)
                    if len(tiles) > 1:
                        a_load(1)
                if n + 2 < len(tiles):
                    a_load(n + 2)
                layer_norm(xt, g1b, b1b, hb[:], tmpA, stA)
                pT = psum_from(P1, c1)
                pTb = pT[:].bitcast(BF16)
                for c in range(8):
                    T.op("pe", "transpose", out=pTb[:, c * 128:(c + 1) * 128], in_=hb[:, c * 128:(c + 1) * 128],
                         identity=identb[:])
                T.op("act", "copy", out=hT[:].rearrange("p c t -> p (c t)"), in_=pTb[:, 0:1024])
                for half in range(2):
                    pz = psum_from(P1, c1)
                    for k in range(8):
                        T.op("pe", "matmul", out=pz[:, 0:512], lhsT=hT[:, k, :],
                             rhs=win[:, k, half * 512:(half + 1) * 512], start=(k == 0), stop=(k == 7))
                    T.op("act", "activation", out=zs_[:, half * 512:(half + 1) * 512], in_=pz[:, 0:512], func=AF.Silu)
                p1 = psum_from(P1, c1)
                for k in range(8):
                    T.op("pe", "matmul", out=p1[:, 0:400], lhsT=hT[:, k, :], rhs=win[:, k, 2560:2960],
                         start=(k == 0), stop=(k == 7))
                p2 = psum_from(P1, c1)
                for k in range(8):
                    T.op("pe", "matmul", out=p2[:, 0:320], lhsT=hT[:, k, :], rhs=win[:, k, 2960:3280],
                         start=(k == 0), stop=(k == 7))
                T.op("dve", "tensor_tensor", out=dtv[:], in0=p1[:, 0:16], in1=dtb[:], op=ALU.add)
                T.op("dve", "tensor_scalar", out=dtw[:, 0, :], in0=dtv[:], scalar1=-1.0, scalar2=None, op0=ALU.mult)
                T.op("dve", "tensor_tensor", out=dtw[:, 3, :], in0=dtv[:], in1=dtw[:, 0, :], op=ALU.min)
                T.op("act", "activation", out=dtw[:, 1, :], in_=dtw[:, 3, :], func=AF.Exp)
                T.op("act", "activation", out=dtw[:, 2, :], in_=dtw[:, 1, :], func=AF.Ln, bias=1.0)
                T.op("dve", "scalar_tensor_tensor", out=dt_[:], in0=dtv[:], scalar=0.0, in1=dtw[:, 2, :],
                     op0=ALU.max, op1=ALU.add)
                T.op("dve", "tensor_tensor", out=a_[:], in0=dt_[:], in1=Ab[:], op=ALU.mult)
                T.op("pool", "memset", ap=ssq1[:], constant=0.0)
                T.op("act", "activation", out=junk1[:, 0:384], in_=p1[:, 16:400], func=AF.Square,
                     accum_out=ssq1[:, 0:1])
                T.op("act", "activation", out=junk1[:, 384:640], in_=p2[:, 0:256], func=AF.Square,
                     accum_out=ssq1[:, 1:2])
                T.op("act", "activation", out=ssq1[:, 2:3], in_=ssq1[:, 0:1], func=AF.Sqrt, bias=EPS, scale=1.0 / 384)
                T.op("act", "activation", out=ssq1[:, 3:4], in_=ssq1[:, 1:2], func=AF.Sqrt, bias=EPS, scale=1.0 / 256)
                T.op("dve", "reciprocal", out=ssq1[:, 4:6], in_=ssq1[:, 2:4])
                T.op("dve", "tensor_scalar", out=latn[:, 0:384], in0=p1[:, 16:400], scalar1=ssq1[:, 4:5],
                     scalar2=None, op0=ALU.mult)
                T.op("dve", "tensor_scalar", out=latn[:, 384:640], in0=p2[:, 0:256], scalar1=ssq1[:, 5:6],
                     scalar2=None, op0=ALU.mult)
                T.op("act", "copy", out=kp[:], in_=p2[:, 256:320])
                cs, sn = cosT[:, tt, :], sinT[:, tt, :]
                T.op("dve", "tensor_tensor", out=kpw[:, 0, :], in0=kp[:, 0:32], in1=cs, op=ALU.mult)
                T.op("dve", "tensor_tensor", out=kpw[:, 1, :], in0=kp[:, 32:64], in1=sn, op=ALU.mult)
                T.op("dve", "tensor_tensor", out=kpw[:, 2, :], in0=kp[:, 32:64], in1=cs, op=ALU.mult)
                T.op("dve", "tensor_tensor", out=kpw[:, 3, :], in0=kp[:, 0:32], in1=sn, op=ALU.mult)
                T.op("dve", "tensor_tensor", out=kpr[:, 0:32], in0=kpw[:, 0, :], in1=kpw[:, 1, :], op=ALU.subtract)
                T.op("dve", "tensor_tensor", out=kpr[:, 32:64], in0=kpw[:, 2, :], in1=kpw[:, 3, :], op=ALU.add)
                pL = psum_from(P1, c1)
                pLb = pL[:].bitcast(BF16)
                for c in range(5):
                    T.op("pe", "transpose", out=pLb[:, c * 128:(c + 1) * 128], in_=latn[:, c * 128:(c + 1) * 128],
                         identity=identb[:])
                T.op("pe", "transpose", out=pLb[0:64, 640:768], in_=kpr[:, 0:64], identity=identb[:])
                T.op("dve", "tensor_copy", out=lat_sb[:, 0:640], in_=pLb[:, 0:640])
                T.op("dve", "tensor_copy", out=lat_sb[0:64, 640:768], in_=pLb[0:64, 640:768])
                T.dma("sp", out=QLs[seq, :, :, ts], in_=lat_sb[:, 0:384].rearrange("p (c t) -> p c t", c=3))
                T.dma("sp", out=KVLs[seq, :, :, ts], in_=lat_sb[:, 384:640].rearrange("p (c t) -> p c t", c=2))
                T.dma("sp", out=KPEs[seq, :, ts], in_=lat_sb[0:64, 640:768])
                for b3 in range(3):
                    px = psum_from(P1, c1)
                    for cc in range(4):
                        c = b3 * 4 + cc
                        for k in range(8):
                            T.op("pe", "matmul", out=px[:, cc * 128:(cc + 1) * 128],
                                 lhsT=win[:, k, 1024 + c * 128:1024 + (c + 1) * 128], rhs=hT[:, k, :],
                                 start=(k == 0), stop=(k == 7))
                    T.op("act", "copy", out=xr[:, b3 * 4:(b3 + 1) * 4, 3:131],
                         in_=px[:, 0:512].rearrange("p (c t) -> p c t", c=4))

            def stage2(n):
                seq, tt = tiles[n]
                ts = slice(tt * 128, (tt + 1) * 128)
                zs_, dt_, a_, xr = zs2[n % 2], dt2[n % 2], a2[n % 2], xraw2[n % 2]
                if tt == 0:
                    T.op("pool", "memset", ap=xr[:, :, 0:3], constant=0.0)
                    T.op("pool", "memset", ap=prevT[:], constant=0.0)
                    T.op("pool", "memset", ap=prevb[:], constant=0.0)
                T.op("pool", "memset", ap=ssq2[:], constant=0.0)
                T.op("dve", "tensor_tensor", out=xacc[:], in0=xr[:, :, 0:128],
                     in1=cw[:, :, 0:1].to_broadcast([128, 12, 128]), op=ALU.mult)
                for k in range(1, 4):
                    T.op("pool", "tensor_tensor", out=xtmp[:], in0=xr[:, :, k:k + 128],
                         in1=cw[:, :, k:k + 1].to_broadcast([128, 12, 128]), op=ALU.mult)
                    T.op("dve", "tensor_tensor", out=xacc[:], in0=xacc[:], in1=xtmp[:], op=ALU.add)
                T.op("pool", "tensor_tensor", out=xacc[:], in0=xacc[:],
                     in1=cb[:].unsqueeze(2).to_broadcast([128, 12, 128]), op=ALU.add)
                T.op("act", "activation", out=xact[:], in_=xacc[:], func=AF.Silu)
                if tt + 1 < NT:
                    T.op("pool", "tensor_copy", out=xraw2[(n + 1) % 2][:, :, 0:3], in_=xr[:, :, 128:131])
                T.op("dve", "tensor_copy", out=bcb[:], in_=xact[:, 8:12, :])
                for b2 in range(2):
                    pt = psum_from(P2, c2)
                    for cc in range(4):
                        T.op("pe", "transpose", out=pt[:, cc * 128:(cc + 1) * 128], in_=xact[:, b2 * 4 + cc, :],
                             identity=identf[:])
                    T.op("act", "copy", out=xs_tok[:, b2 * 512:(b2 + 1) * 512], in_=pt[:, 0:512])
                pt = psum_from(P2, c2)
                for cc in range(2):
                    T.op("pe", "transpose", out=pt[:, cc * 128:(cc + 1) * 128], in_=xact[:, 8 + cc, :],
                         identity=identf[:])
                T.op("act", "copy", out=btok[:], in_=pt[:, 0:256])
                for g in range(2):
                    hs = slice(g * 8, (g + 1) * 8)
                    fs = slice(g * 512, (g + 1) * 512)
                    a_g = a_[:, hs].unsqueeze(2).to_broadcast([128, 8, 128])
                    T.op("dve", "tensor_tensor", out=Rm[:], in0=a_g,
                         in1=triu[:].unsqueeze(1).to_broadcast([128, 8, 128]), op=ALU.mult)
                    T.op("pool", "tensor_copy", out=Abc[:], in_=a_g)
                    for j in range(2):
                        pa_ = psum_from(P2, c2)
                        T.op("pe", "matmul", out=pa_[:, 0:512], lhsT=onesf[:],
                             rhs=Rm[:, 4 * j:4 * j + 4, :].rearrange("p a b -> p (a b)"), start=True, stop=True)
                        T.op("act", "activation", out=expA[:, 4 * j:4 * j + 4, :].rearrange("p a b -> p (a b)"),
                             in_=pa_[:, 0:512], func=AF.Exp)
                        pe_ = psum_from(P2, c2)
                        T.op("pe", "matmul", out=pe_[:, 0:512], lhsT=onesf[:],
                             rhs=Rm[:, 4 * j:4 * j + 4, :].rearrange("p a b -> p (a b)"), start=True, stop=False)
                        T.op("pe", "matmul", out=pe_[:, 0:512], lhsT=ntriu[:],
                             rhs=Abc[:, 4 * j:4 * j + 4, :].rearrange("p a b -> p (a b)"), start=False, stop=False)
                        T.op("pe", "matmul", out=pe_[:, 0:512], lhsT=identf[:], rhs=negm4[:], start=False, stop=True)
                        T.op("act", "activation", out=decT[:, 4 * j:4 * j + 4, :].rearrange("p a b -> p (a b)"),
                             in_=pe_[:, 0:512], func=AF.Exp)
                    pc = psum_from(P2, c2)
                    T.op("pe", "matmul", out=pc[:, 0:128], lhsT=bcb[:, g, :], rhs=bcb[:, 2 + g, :], start=True, stop=True)
                    T.op("act", "copy", out=cb_sb[:], in_=pc[:, 0:128])
                    T.op("dve", "tensor_tensor", out=MT[:], in0=decT[:],
                         in1=cb_sb[:].unsqueeze(1).to_broadcast([128, 8, 128]), op=ALU.mult)
                    T.op("pool", "tensor_tensor", out=CsT[:], in0=expA[:],
                         in1=xact[:, 10 + g, :].unsqueeze(1).to_broadcast([128, 8, 128]), op=ALU.mult)
                    xs_g = xs_tok[:, fs].rearrange("p (h d) -> p h d", h=8)
                    T.op("dve", "tensor_tensor", out=xdt[:], in0=xs_g,
                         in1=dt_[:, hs].unsqueeze(2).to_broadcast([128, 8, 64]), op=ALU.mult)
                    T.op("dve", "tensor_tensor", out=w2[:], in0=dt_[:, hs], in1=decT[:, :, 127], op=ALU.mult)
                    T.op("pool", "tensor_tensor", out=xdtd[:], in0=xs_g,
                         in1=w2[:].unsqueeze(2).to_broadcast([128, 8, 64]), op=ALU.mult)
                    py = psum_from(P2, c2)
                    for j in range(8):
                        T.op("pe", "matmul", out=py[:, j * 64:(j + 1) * 64], lhsT=MT[:, j, :], rhs=xdt[:, j, :],
                             start=True, stop=False)
                        T.op("pe", "matmul", out=py[:, j * 64:(j + 1) * 64], lhsT=CsT[:, j, :],
                             rhs=prevb[:, g * 512 + j * 64:g * 512 + (j + 1) * 64], start=False, stop=True)
                    pst = psum_from(P2, c2)
                    for j in range(8):
                        T.op("pe", "matmul", out=pst[:, j * 64:(j + 1) * 64], lhsT=btok[:, g * 128:(g + 1) * 128],
                             rhs=xdtd[:, j, :], start=True, stop=True)
                    T.op("pool", "tensor_tensor", out=ytmp[:].rearrange("p (h d) -> p h d", h=8), in0=xs_g,
                         in1=dskb[:, hs].unsqueeze(2).to_broadcast([128, 8, 64]), op=ALU.mult)
                    T.op("dve", "tensor_tensor", out=ysb[:, fs], in0=ytmp[:], in1=py[:, 0:512], op=ALU.add)
                    pv_g = prevT[:, fs].rearrange("p (h d) -> p h d", h=8)
                    T.op("dve", "tensor_tensor", out=pv_g, in0=pv_g,
                         in1=expA[:, :, 127:128].to_broadcast([128, 8, 64]), op=ALU.mult)
                    T.op("dve", "tensor_tensor", out=prevT[:, fs], in0=prevT[:, fs], in1=pst[:, 0:512], op=ALU.add)
                    T.op("act", "copy", out=prevb[:, fs], in_=prevT[:, fs])
                T.op("dve", "tensor_tensor", out=ysb[:], in0=ysb[:], in1=zs_[:], op=ALU.mult)
                T.op("act", "activation", out=junk2[:], in_=ysb[:], func=AF.Square, accum_out=ssq2[:, 6:7])
                T.op("act", "activation", out=ssq2[:, 7:8], in_=ssq2[:, 6:7], func=AF.Sqrt, bias=EPS, scale=1.0 / 1024)
                T.op("dve", "reciprocal", out=ssq2[:, 8:9], in_=ssq2[:, 7:8])
                T.op("dve", "tensor_scalar", out=ynb[:], in0=ysb[:], scalar1=ssq2[:, 8:9], scalar2=None,
                     op0=ALU.mult)
                pY = psum_from(P2, c2)
                pYb = pY[:].bitcast(BF16)
                for c in range(8):
                    T.op("pe", "transpose", out=pYb[:, c * 128:(c + 1) * 128], in_=ynb[:, c * 128:(c + 1) * 128],
                         identity=identb[:])
                T.op("act", "copy", out=ysT[:].rearrange("p c t -> p (c t)"), in_=pYb[:, 0:1024])
                T.dma("sp", out=YS[seq, :, :, ts], in_=ysT[:])

            stage1(0)
            for n in range(len(tiles)):
                if n + 1 < len(tiles):
                    stage1(n + 1)
                stage2(n)


        if stop_after == "A":
            T.finish("sp", ["s_ys", "s_ql", "s_kvl", "s_kpe"])
            ln12.close()
            return nc

        SCALE = 192.0 ** -0.5
        NQG = S // 512
        with ExitStack() as pb:
            wqs = sb("wqs", [128, 3, 1536], F32, pb)
            wq = sb("wq", [128, 3, 1536], BF16, pb)
            wqr = sb("wqr", [128, 3, 8, 64], BF16, pb)
            gq = sb("gq", [128, 3], F32, pb)
            gkv = sb("gkv", [128, 2], F32, pb)
            wkv = sb("wkv", [128, 2, 2048], BF16, pb)
            T.dma("sp", out=wqs[:], in_=w_q_b.rearrange("(c p) n -> p c n", p=128))
            T.dma("sp", out=gq[:].unsqueeze(2), in_=q_norm_g.rearrange("(c p o) -> p c o", p=128, o=1),
                  allow_slow_non_contiguous=True)
            T.dma("sp", out=gkv[:].unsqueeze(2), in_=kv_norm_g.rearrange("(c p o) -> p c o", p=128, o=1),
                  allow_slow_non_contiguous=True)
            for c in range(3):
                T.op("dve", "tensor_scalar", out=wq[:, c, :], in0=wqs[:, c, :], scalar1=gq[:, c:c + 1], scalar2=None,
                     op0=ALU.mult)
            wq4 = wq[:].rearrange("p c (h d) -> p c h d", h=8)
            for c in range(3):
                T.op("dve", "tensor_scalar", out=wqr[:, c, :, 0:32], in0=wq4[:, c, :, 160:192], scalar1=-1.0, scalar2=None,
                     op0=ALU.mult)
                T.op("pool", "tensor_copy", out=wqr[:, c, :, 32:64], in_=wq4[:, c, :, 128:160])
            for c in range(2):
                T.dma("sp", out=wqs[:, 0, :], in_=w_kv_b[c * 128:(c + 1) * 128, 0:1536])
                T.dma("sp", out=wqs[:, 1, 0:512], in_=w_kv_b[c * 128:(c + 1) * 128, 1536:2048])
                T.op("dve", "tensor_scalar", out=wkv[:, c, 0:1536], in0=wqs[:, 0, :], scalar1=gkv[:, c:c + 1],
                     scalar2=None, op0=ALU.mult)
                T.op("dve", "tensor_scalar", out=wkv[:, c, 1536:2048], in0=wqs[:, 1, 0:512], scalar1=gkv[:, c:c + 1],
                     scalar2=None, op0=ALU.mult)
            cos2 = sb("cos2", [64, S], F32, pb)
            sin2 = sb("sin2", [64, S], F32, pb)
            T.dma("sp", out=cos2[:], in_=c_cos2t)
            T.dma("sp", out=sin2[:], in_=c_sin2t)
            triub = sb("triub", [128, 128], BF16, pb)
            T.op("dve", "tensor_copy", out=triub[:], in_=triu[:])
            QL = sb("QL", [128, 3, S], BF16, pb)
            KVL = sb("KVL", [128, 2, S], BF16, pb)
            KPE = sb("KPE", [128, S], BF16, pb)
            QN = sb("QN", [128, S], BF16, pb)
            QP = sb("QP", [128, S], BF16, pb)
            T.op("pool", "memset", ap=KPE[64:128, :], constant=0.0)
            T.op("pool", "memset", ap=QP[64:128, :], constant=0.0)
            KN = sb("KN", [128, S], BF16, pb)
            Vt = sb("Vt", [128, NT, 129], BF16, pb)
            rt1 = sb("rt1", [64, 512], F32, pb)
            rt2 = sb("rt2", [64, 512], F32, pb)
            ptb = [sb("ptb%d" % i, [128, 512], BF16, pb) for i in range(3)]
            rcp = sb("rcp", [128, 4], F32, pb)
            ob = sb("ob", [128, 128], BF16, pb)
            ymT = [sb("ymT%d" % i, [128, 512], BF16, pb) for i in range(2)]
            T.op("pool", "memset", ap=Vt[:, :, 128:129], constant=1.0)
            OB = [0, 1, 2, 3]
            SP_ = [4, 5, 6, 7]
            scur = [0]
            pcount = 0
            for seq in range(NSEQ):
                T.dma("sp", out=QL[:], in_=QLs[seq])
                T.dma("sp", out=KVL[:], in_=KVLs[seq])
                T.dma("sp", out=KPE[0:64, :], in_=KPEs[seq])
                for h in range(8):
                    qc = h * 192
                    kc = h * 256
                    for blk in range(NQG):
                        bs = slice(blk * 512, (blk + 1) * 512)
                        p = psum_from(SP_, scur)
                        for c in range(3):
                            T.op("pe", "matmul", out=p[:, 0:512], lhsT=wq[:, c, qc:qc + 128], rhs=QL[:, c, bs],
                                 start=(c == 0), stop=(c == 2))
                        T.op("act", "copy", out=QN[:, bs], in_=p[:, 0:512])
                        p = psum_from(SP_, scur)
                        for c in range(3):
                            T.op("pe", "matmul", out=p[0:64, 0:512], lhsT=wq[:, c, qc + 128:qc + 192], rhs=QL[:, c, bs],
                                 start=(c == 0), stop=(c == 2))
                        T.op("dve", "tensor_tensor", out=rt1[:], in0=p[0:64, 0:512], in1=cos2[:, bs], op=ALU.mult)
                        p = psum_from(SP_, scur)
                        for c in range(3):
                            T.op("pe", "matmul", out=p[0:64, 0:512], lhsT=wqr[:, c, h, :], rhs=QL[:, c, bs],
                                 start=(c == 0), stop=(c == 2))
                        T.op("dve", "tensor_tensor", out=rt2[:], in0=p[0:64, 0:512], in1=sin2[:, bs], op=ALU.mult)
                        T.op("pool", "tensor_tensor", out=QP[0:64, bs], in0=rt1[:], in1=rt2[:], op=ALU.add)
                        p = psum_from(SP_, scur)
                        for c in range(2):
                            T.op("pe", "matmul", out=p[:, 0:512], lhsT=wkv[:, c, kc:kc + 128], rhs=KVL[:, c, bs],
                                 start=(c == 0), stop=(c == 1))
                        T.op("dve", "tensor_copy", out=KN[:, bs], in_=p[:, 0:512])
                        p = psum_from(SP_, scur)
                        for t4 in range(4):
                            tsl = slice(blk * 512 + t4 * 128, blk * 512 + (t4 + 1) * 128)
                            for c in range(2):
                                T.op("pe", "matmul", out=p[:, t4 * 128:(t4 + 1) * 128], lhsT=KVL[:, c, tsl],
                                     rhs=wkv[:, c, kc + 128:kc + 256], start=(c == 0), stop=(c == 1))
                        T.op("act", "copy", out=Vt[:, blk * 4:(blk + 1) * 4, 0:128],
                             in_=p[:, 0:512].rearrange("p (t d) -> p t d", t=4))
                    for qg in range(NQG):
                        qs = slice(qg * 512, (qg + 1) * 512)
                        nkt = 4 * qg + 4
                        def s_mm(kt_):
                            ks_ = slice(kt_ * 128, (kt_ + 1) * 128)
                            p_ = psum_from(SP_, scur)
                            T.op("pe", "matmul", out=p_[:, 0:512], lhsT=KN[:, ks_], rhs=QN[:, qs], start=True, stop=False)
                            T.op("pe", "matmul", out=p_[:, 0:512], lhsT=KPE[:, ks_], rhs=QP[:, qs], start=False, stop=True)
                            T.pe_signal()
                            return p_
                        p_next = s_mm(0)
                        for kt in range(nkt):
                            p = p_next
                            if kt + 1 < nkt:
                                p_next = s_mm(kt + 1)
                            jmin = max(0, kt - 4 * qg)
                            pt = ptb[pcount % 3]
                            pcount += 1
                            T.op("act", "activation", out=pt[:, jmin * 128:512], in_=p[:, jmin * 128:512], func=AF.Exp,
                                 scale=SCALE)
                            if kt >= 4 * qg:
                                T.op("pool", "tensor_tensor", out=pt[:, jmin * 128:(jmin + 1) * 128],
                                     in0=pt[:, jmin * 128:(jmin + 1) * 128], in1=triub[:], op=ALU.mult)
                            for j in range(jmin, 4):
                                T.op("pe", "matmul", out=PS[OB[j]][:, 0:129], lhsT=pt[:, j * 128:(j + 1) * 128],
                                     rhs=Vt[:, kt, :], start=(kt == 0), stop=(kt == 4 * qg + j))
                            T.pe_signal()
                        ym = ymT[qg % 2]
                        for j in range(4):
                            T.op("dve", "reciprocal", out=rcp[:, j:j + 1], in_=PS[OB[j]][:, 128:129])
                            T.op("dve", "tensor_scalar", out=ob[:], in0=PS[OB[j]][:, 0:128], scalar1=rcp[:, j:j + 1],
                                 scalar2=None, op0=ALU.mult)
                            p = psum_from(SP_, scur)
                            pbf = p[:].bitcast(BF16)
                            T.op("pe", "transpose", out=pbf[:, 0:128], in_=ob[:], identity=identb[:])
                            T.op("act", "copy", out=ym[:, j * 128:(j + 1) * 128], in_=pbf[:, 0:128])
                        T.dma("sp", out=YM[seq, :, h, qs], in_=ym[:])

        if stop_after == "B":
            T.finish("sp", ["s_ys", "s_ql", "s_kvl", "s_kpe", "s_ym"])
            ln12.close()
            return nc

        H2s = nc.dram_tensor("s_h2", [NTOK, D], F32, kind=skind).ap()
        H2Ts = nc.dram_tensor("s_h2t", [128, 8, NTOK], BF16, kind="Internal").ap()
        SCs = nc.dram_tensor("s_sc", [NTOK // 128, 128, 16, 128], F32, kind="Internal").ap()
        SMs = nc.dram_tensor("s_sm", [NTOK // 128, 128, 8, 33], F32, kind="Internal").ap()

        with ExitStack() as pt_:
            NBUF = 3
            tstV = [sb("tstV%d" % i, [128, 4, D], F32, pt_) for i in range(NBUF)]
            tstU = [sb("tstU%d" % i, [128, 4, D], F32, pt_) for i in range(NBUF)]
            tbfV = [sb("tbfV%d" % i, [128, 4, D], BF16, pt_) for i in range(2)]
            tbfU = [sb("tbfU%d" % i, [128, 4, D], BF16, pt_) for i in range(2)]
            utile = [sb("utile%d" % i, [128, 8, 512], BF16, pt_) for i in range(2)]
            u_v = u_table.rearrange("(n c p) d -> n p c d", c=4, p=128)
            v_v = v_table.rearrange("(n c p) d -> n p c d", c=4, p=128)
            vb_v = VB.rearrange("(n c p) d -> n p c d", c=4, p=128)
            ut_v = UT.rearrange("k p e -> p k e")
            NPR = NEXP // 512

            def prep_load(n):
                T.dma("sp", out=tstV[n % NBUF][:], in_=v_v[n])
                T.dma("sp", out=tstU[n % NBUF][:], in_=u_v[n])

            prep_load(0)
            prep_load(1)
            for n in range(NPR):
                if n + 2 < NPR:
                    prep_load(n + 2)
                bfv, bfu = tbfV[n % 2], tbfU[n % 2]
                cast_copy(bfv[:], tstV[n % NBUF][:])
                T.dma("sp", out=vb_v[n], in_=bfv[:], _w=["s_vb%d" % n])
                cast_copy(bfu[:], tstU[n % NBUF][:])
                ut_ = utile[n % 2]
                for c4 in range(4):
                    p = psum()
                    pbf = p[:].bitcast(BF16)
                    for k in range(8):
                        T.op("pe", "transpose", out=pbf[:, k * 128:(k + 1) * 128], in_=bfu[:, c4, k * 128:(k + 1) * 128],
                             identity=identb[:])
                    T.pe_signal()
                    if c4 % 2 == 0:
                        T.op("act", "copy", out=ut_[:, :, c4 * 128:(c4 + 1) * 128],
                             in_=pbf[:, 0:1024].rearrange("p (k e) -> p k e", k=8))
                    else:
                        T.op("dve", "tensor_copy", out=ut_[:, :, c4 * 128:(c4 + 1) * 128],
                             in_=pbf[:, 0:1024].rearrange("p (k e) -> p k e", k=8))
                T.dma("sp", out=ut_v[:, :, n * 512:(n + 1) * 512], in_=ut_[:], _w=["s_ut%d" % n])

        with ExitStack() as pc_:
            wout = sb("wout", [128, 16, D], BF16, pc_)
            gs = sb("gs", [128, 8], F32, pc_)
            T.dma("sp", out=gs[:].unsqueeze(2), in_=ssm_norm_g.rearrange("(c p o) -> p c o", p=128, o=1),
                  allow_slow_non_contiguous=True)
            wst = [sb("wst%d" % i, [128, 2, D], F32, pc_) for i in range(2)]
            wo_v = w_out.rearrange("(k p) n -> p k n", p=128)
            for b in range(8):
                st_ = wst[b % 2]
                T.dma("sp", out=st_[:], in_=wo_v[:, 2 * b:2 * b + 2, :])
                for kk in range(2):
                    k = 2 * b + kk
                    if k < 8:
                        T.op("dve", "tensor_scalar", out=wout[:, k, :], in0=st_[:, kk, :], scalar1=gs[:, k:k + 1],
                             scalar2=None, op0=ALU.mult)
                    else:
                        cast_copy(wout[:, k, :], st_[:, kk, :])
            wqy = sb("wqy", [128, 8, 2048], BF16, pc_)
            KT = sb("KT", [128, 16, 128], BF16, pc_)
            with ExitStack() as pw:
                qst = [sb("qst%d" % i, [128, 8, 256], F32, pw) for i in range(2)]
                wq_v = w_query.rearrange("(k p) n -> p k n", p=128)
                for b in range(8):
                    st_ = qst[b % 2]
                    T.dma("sp", out=st_[:], in_=wq_v[:, :, b * 256:(b + 1) * 256])
                    cast_copy(wqy[:, :, b * 256:(b + 1) * 256], st_[:])
                skf = sb("skf", [128, 16, 128], F32, pw)
                skb = sb("skb", [128, 16, 128], BF16, pw)
                T.dma("sp", out=skf[:], in_=sub_keys.rearrange("k n d -> n k d"))
                T.op("dve", "tensor_copy", out=skb[:], in_=skf[:])
                for b in range(2):
                    p = psum()
                    pbf = p[:].bitcast(BF16)
                    for j in range(8):
                        T.op("pe", "transpose", out=pbf[:, j * 128:(j + 1) * 128], in_=skb[:, b * 8 + j, :],
                             identity=identb[:])
                    T.op("act", "copy", out=KT[:, b * 8:(b + 1) * 8, :].rearrange("p a b -> p (a b)"), in_=pbf[:, 0:1024])
                T.barrier()
            qT = sb("qT", [128, 16, 128], BF16, pc_)
            SCt2 = [sb("SCt%d" % i, [128, 16, 128], F32, pc_) for i in range(2)]
            SMt2 = [sb("SMt%d" % i, [128, 8, 33], F32, pc_) for i in range(2)]
            scwL = [sb("scw%d" % i, [128, 128], F32, pc_) for i in range(16)]
            svA = [sb("svA%d" % i, [128, 8], F32, pc_) for i in range(16)]
            svB = [sb("svB%d" % i, [128, 8], F32, pc_) for i in range(16)]
            cwL = [sb("cw%d" % i, [128, 256], F32, pc_) for i in range(8)]
            tpA = [sb("tpA%d" % i, [128, 8], F32, pc_) for i in range(8)]
            tpB = [sb("tpB%d" % i, [128, 8], F32, pc_) for i in range(8)]
            sv = sb("sv", [128, 16, 16], F32, pc_)
            cand = sb("cand", [128, 8, 256], F32, pc_)
            top = sb("top", [128, 8, 16], F32, pc_)
            exz = sb("exz", [128, 8, 16], F32, pc_)
            zz = sb("zz", [128, 16], F32, pc_)
            xc = [sb("xc%d" % i, [128, D], F32, pc_) for i in range(2)]
            yst = [sb("yst%d" % i, [128, 8, 128], BF16, pc_) for i in range(2)]
            ymt = [sb("ymt%d" % i, [128, 8, 128], BF16, pc_) for i in range(2)]
            tmpC = sb("tmpC", [128, D], F32, pc_)
            stC = sb("stC", [128, 16], F32, pc_)
            hC = sb("hC", [128, D], F32, pc_)
            rC = sb("rC", [128, D], F32, pc_)
            h2 = [sb("h2_%d" % i, [128, D], F32, pc_) for i in range(2)]
            h2b = sb("h2b", [128, D], BF16, pc_)
            h2T = [sb("h2T%d" % i, [128, 8, 128], BF16, pc_) for i in range(2)]
            def c_loads(ti_):
                seq_, tt_ = ti_ // NT, ti_ % NT
                ts_ = slice(tt_ * 128, (tt_ + 1) * 128)
                T.dma("sp", out=xc[ti_ % 2][:], in_=x[seq_, ts_, :])
                T.dma("sp", out=yst[ti_ % 2][:], in_=YS[seq_, :, :, ts_])
                T.dma("sp", out=ymt[ti_ % 2][:], in_=YM[seq_, :, :, ts_])

            for seq in range(NSEQ):
                for tt in range(NT):
                    ti = seq * NT + tt
                    ts = slice(tt * 128, (tt + 1) * 128)
                    gsl = slice(ti * 128, (ti + 1) * 128)
                    xt, ys_, ym_ = xc[ti % 2], yst[ti % 2], ymt[ti % 2]
                    if ti == 0:
                        c_loads(0)
                    if ti + 1 < NSEQ * NT:
                        c_loads(ti + 1)
                    layer_norm(xt, g1b, b1b, hC[:], tmpC, stC)
                    for half in range(2):
                        p = psum()
                        for k in range(16):
                            lhs = ys_[:, k, :] if k < 8 else ym_[:, k - 8, :]
                            T.op("pe", "matmul", out=p[:, 0:512], lhsT=lhs, rhs=wout[:, k, half * 512:(half + 1) * 512],
                                 start=(k == 0), stop=(k == 15))
                        T.op("dve", "scalar_tensor_tensor", out=rC[:, half * 512:(half + 1) * 512],
                             in0=hC[:, half * 512:(half + 1) * 512], scalar=ALPHA, in1=p[:, 0:512],
                             op0=ALU.mult, op1=ALU.add)
                    h2_ = h2[ti % 2]
                    layer_norm(rC, g2b, b2b, h2_[:], tmpC, stC)
                    T.dma("sp", out=H2s[gsl, :], in_=h2_[:], _w=["s_h2_%d" % ti])
                    T.op("act", "copy", out=h2b[:], in_=h2_[:])
                    p = psum()
                    pbf = p[:].bitcast(BF16)
                    for c in range(8):
                        T.op("pe", "transpose", out=pbf[:, c * 128:(c + 1) * 128], in_=h2b[:, c * 128:(c + 1) * 128],
                             identity=identb[:])
                    h2T_ = h2T[ti % 2]
                    T.op("dve", "tensor_copy", out=h2T_[:].rearrange("p c t -> p (c t)"), in_=pbf[:, 0:1024])
                    T.dma("sp", out=H2Ts[:, :, gsl], in_=h2T_[:], _w=["s_h2t_%d" % ti])
                    SCt, SMt = SCt2[ti % 2], SMt2[ti % 2]
                    for b in range(4):
                        p = psum()
                        for j in range(4):
                            hk = b * 4 + j
                            for k in range(8):
                                T.op("pe", "matmul", out=p[:, j * 128:(j + 1) * 128], lhsT=wqy[:, k, hk * 128:(hk + 1) * 128],
                                     rhs=h2T_[:, k, :], start=(k == 0), stop=(k == 7))
                        T.op("act", "copy", out=qT[:, b * 4:(b + 1) * 4, :].rearrange("p a b -> p (a b)"), in_=p[:, 0:512])
                    for b in range(4):
                        p = psum()
                        for j in range(4):
                            hk = b * 4 + j
                            T.op("pe", "matmul", out=p[:, j * 128:(j + 1) * 128], lhsT=qT[:, hk, :], rhs=KT[:, hk, :],
                                 start=True, stop=True)
                        T.op("act", "copy", out=SCt[:, b * 4:(b + 1) * 4, :].rearrange("p a b -> p (a b)"), in_=p[:, 0:512])
                    T.dma("sp", out=SCs[ti], in_=SCt[:], _w=["s_sc_%d" % ti])
                    for hk in range(16):
                        T.op("dve", "max", out=svA[hk][:], in_=SCt[:, hk, :], _r=["SCt%d_%d" % (ti % 2, hk)])
                    for hk in range(16):
                        T.op("dve", "match_replace", out=scwL[hk][:], in_to_replace=svA[hk][:], in_values=SCt[:, hk, :],
                             imm_value=-1e30)
                    for hk in range(16):
                        T.op("dve", "max", out=svB[hk][:], in_=scwL[hk][:])
                    for hk in range(16):
                        T.op("pool", "tensor_copy", out=sv[:, hk, 0:8], in_=svA[hk][:])
                        T.op("pool", "tensor_copy", out=sv[:, hk, 8:16], in_=svB[hk][:])
                    sv4 = sv[:].rearrange("p (h k) a -> p h k a", k=2)
                    T.op("pool", "tensor_tensor", out=cand[:].rearrange("p h (a b) -> p h a b", a=16),
                         in0=sv4[:, :, 0, :].unsqueeze(3).to_broadcast([128, 8, 16, 16]),
                         in1=sv4[:, :, 1, :].unsqueeze(2).to_broadcast([128, 8, 16, 16]), op=ALU.add)
                    for h in range(8):
                        T.op("dve", "max", out=tpA[h][:], in_=cand[:, h, :])
                    for h in range(8):
                        T.op("dve", "match_replace", out=cwL[h][:], in_to_replace=tpA[h][:], in_values=cand[:, h, :],
                             imm_value=-1e30)
                    for h in range(8):
                        T.op("dve", "max", out=tpB[h][:], in_=cwL[h][:])
                    for h in range(8):
                        T.op("pool", "tensor_copy", out=top[:, h, 0:8], in_=tpA[h][:])
                        T.op("pool", "tensor_copy", out=top[:, h, 8:16], in_=tpB[h][:])
                    T.op("dve", "tensor_tensor", out=SMt[:, :, 0:16], in0=top[:, :, 15:16].to_broadcast([128, 8, 16]),
                         in1=sv4[:, :, 0, :], op=ALU.subtract)
                    T.op("pool", "tensor_copy", out=SMt[:, :, 16:32], in_=sv4[:, :, 0, :])
                    T.op("dve", "tensor_tensor", out=exz[:], in0=top[:],
                         in1=top[:, :, 15:16].to_broadcast([128, 8, 16]), op=ALU.subtract)
                    T.op("act", "activation", out=exz[:], in_=exz[:], func=AF.Exp)
                    T.op("dve", "tensor_reduce", out=zz[:, 0:8], in_=exz[:], axis=AX.X, op=ALU.add)
                    T.op("dve", "reciprocal", out=SMt[:, :, 32], in_=zz[:, 0:8])
                    T.dma("sp", out=SMs[ti], in_=SMt[:], _w=["s_sm_%d" % ti])

        if stop_after == "C":
            T.finish("sp", ["s_h2", "s_h2t", "s_ut", "s_vb", "s_sc", "s_sm"] + ["s_h2_%d" % i for i in range(NSEQ * NT)])
            ln12.close()
            return nc

        ln12.close()
        TG = 2
        IC = 4
        NG = NTOK // (TG * 128)
        NIC = 128 // IC
        g3b = bcast_load("g3b", ln_ffn_g, D)
        b3b = bcast_load("b3b", ln_ffn_b, D)
        ut_v = UT.rearrange("k p e -> p k e")
        vb_v2 = VB.rearrange("(n c p) d -> n p c d", c=IC, p=128)
        OPB = [0, 1, 2, 3]
        CP = [4, 5, 6, 7]
        ATP = [4, 5]
        with ExitStack() as pd:
            H2T = sb("H2T", [128, 8, TG * 128], BF16, pd)
            GT = [sb("GT%d" % i, [128, 128, 128], BF16, pd) for i in range(TG)]
            ccur, acur = [0], [0]
            ev = 0
            for g in range(NG):
                g0 = g * TG * 128
                nm = lambda s_: "%s_g%d" % (s_, g)
                T.dma("sp", out=H2T[:], in_=H2Ts[:, :, g0:g0 + TG * 128],
                      _r=["s_h2t_%d" % (g * TG + i) for i in range(TG)])
                with ExitStack() as cs:
                    RT = sb(nm("RT"), [128, 128, 128], BF16, cs)
                    WIT = sb(nm("WIT"), [128, 128, 128], BF16, cs)
                    SCt = sb(nm("SCd"), [128, 16, 128], F32, cs)
                    SMt = sb(nm("SMd"), [128, 8, 33], F32, cs)
                    kexp = sb(nm("kexp"), [128, 128], F32, cs)
                    kT = sb(nm("kT"), [128, 128], F32, cs)
                    Xb2 = [sb(nm("Xb%d" % i), [128, 8, 16, 16], F32, cs) for i in range(2)]
                    EXb2 = [sb(nm("EXb%d" % i), [128, 8, 16, 16], F32, cs) for i in range(2)]
                    Rb = [sb(nm("Rb%d" % i), [128, 8, 16, 16], BF16, cs) for i in range(2)]
                    WIb = [sb(nm("WIb%d" % i), [128, 8, 16, 16], BF16, cs) for i in range(2)]
                    for tl in range(TG):
                        ti = g * TG + tl
                        T.dma("sp", out=SCt[:], in_=SCs[ti], _r=["s_sc_%d" % ti])
                        T.dma("sp", out=SMt[:], in_=SMs[ti], _r=["s_sm_%d" % ti])
                        SC4 = SCt[:].rearrange("p (h k) n -> p h k n", k=2)
                        T.op("pool", "tensor_copy", out=kexp[:].rearrange("p (h a) -> p h a", h=8),
                             in_=SMt[:, :, 32:33].to_broadcast([128, 8, 16]))
                        p = psum_from(CP, ccur)
                        T.op("pe", "transpose", out=p[:, 0:128], in_=kexp[:], identity=identf[:])
                        T.pe_signal()
                        T.op("act", "copy", out=kT[:], in_=p[:, 0:128])
                        for jb in range(8):
                            js = slice(jb * 16, (jb + 1) * 16)
                            rb, wb = Rb[jb % 2], WIb[jb % 2]
                            Xb, EXb = Xb2[jb % 2], EXb2[jb % 2]
                            T.op("pool", "tensor_tensor", out=Xb[:],
                                 in0=SC4[:, :, 1, js].unsqueeze(2).to_broadcast([128, 8, 16, 16]),
                                 in1=SMt[:, :, 0:16].unsqueeze(3).to_broadcast([128, 8, 16, 16]), op=ALU.subtract)
                            T.op("act", "activation", out=EXb[:], in_=Xb[:], func=AF.Exp)
                            T.op("dve", "scalar_tensor_tensor", out=rb[:], in0=Xb[:], scalar=-1e-5, in1=EXb[:],
                                 op0=ALU.is_ge, op1=ALU.mult)
                            T.op("dve", "tensor_tensor", out=wb[:],
                                 in0=SC4[:, :, 0, js].unsqueeze(2).to_broadcast([128, 8, 16, 16]),
                                 in1=SMt[:, :, 16:32].unsqueeze(3).to_broadcast([128, 8, 16, 16]), op=ALU.is_equal)
                            rb2 = rb[:].rearrange("p h a j -> p (h a) j")
                            wb2 = wb[:].rearrange("p h a j -> p (h a) j")
                            for b in range(2):
                                c0 = jb * 16 + b * 8
                                p = psum_from(CP, ccur)
                                pbf = p[:].bitcast(BF16)
                                for q in range(8):
                                    T.op("pe", "transpose", out=pbf[:, q * 128:(q + 1) * 128], in_=rb2[:, :, b * 8 + q],
                                         identity=identb[:])
                                T.pe_signal()
                                T.op("dve", "tensor_tensor", out=RT[:, c0:c0 + 8, :],
                                     in0=pbf[:, 0:1024].rearrange("p (j t) -> p j t", j=8),
                                     in1=kT[:].unsqueeze(1).to_broadcast([128, 8, 128]), op=ALU.mult)
                                p = psum_from(CP, ccur)
                                pbf = p[:].bitcast(BF16)
                                for q in range(8):
                                    T.op("pe", "transpose", out=pbf[:, q * 128:(q + 1) * 128], in_=wb2[:, :, b * 8 + q],
                                         identity=identb[:])
                                T.pe_signal()
                                T.op("act", "copy", out=WIT[:, c0:c0 + 8, :],
                                     in_=pbf[:, 0:1024].rearrange("p (j t) -> p j t", j=8))
                        for t4 in range(32):
                            p = psum_from(CP, ccur)
                            for q in range(4):
                                t = t4 * 4 + q
                                T.op("pe", "matmul", out=p[:, q * 128:(q + 1) * 128], lhsT=RT[:, :, t], rhs=WIT[:, :, t],
                                     start=True, stop=True)
                            T.pe_signal()
                            dst = GT[tl][:, t4 * 4:(t4 + 1) * 4, :].rearrange("p t i -> p (t i)")
                            if ev % 2 == 0:
                                T.op("act", "copy", out=dst, in_=p[:, 0:512])
                            else:
                                T.op("dve", "tensor_copy", out=dst, in_=p[:, 0:512])
                            ev += 1
                T.barrier()
                with ExitStack() as es:
                    UTc = [sb(nm("UTc%d" % i), [128, 8, IC * 128], BF16, es) for i in range(2)]
                    Vc = [sb(nm("Vc%d" % i), [128, IC, D], BF16, es) for i in range(2)]
                    gA = [sb(nm("gA%d" % i), [128, IC, TG * 128], BF16, es) for i in range(2)]
                    GAm = [sb(nm("GAm%d" % i), [128, IC, 128], BF16, es) for i in range(4)]
                    h2f = sb(nm("h2f"), [128, D], F32, es)
                    rD = sb(nm("rD"), [128, D], F32, es)
                    tmpD = sb(nm("tmpD"), [128, D], F32, es)
                    stD = sb(nm("stD"), [128, 16], F32, es)
                    yo = [sb(nm("yo%d" % i), [128, D], F32, es) for i in range(2)]

                    def load_tables(ic):
                        T.dma("sp", out=UTc[ic % 2][:], in_=ut_v[:, :, ic * IC * 128:(ic + 1) * IC * 128], _r=["s_ut%d" % ic])
                        T.dma("sp", out=Vc[ic % 2][:], in_=vb_v2[ic], _r=["s_vb%d" % ic])

                    def emit_at(ic):
                        utc, ga = UTc[ic % 2], gA[ic % 2]
                        for c in range(IC):
                            p = psum_from(ATP, acur)
                            for k in range(8):
                                T.op("pe", "matmul", out=p[:, 0:TG * 128], lhsT=utc[:, k, c * 128:(c + 1) * 128],
                                     rhs=H2T[:, k, :], start=(k == 0), stop=(k == 7))
                            T.pe_signal()
                            T.op("act", "activation", out=ga[:, c, :], in_=p[:, 0:TG * 128], func=AF.Gelu)

                    def emit_gv(ic):
                        for tl in range(TG):
                            gam = GAm[(ic * TG + tl) % 4]
                            T.op("dve", "tensor_tensor", out=gam[:],
                                 in0=GT[tl][:, :, ic * IC:(ic + 1) * IC].rearrange("p t c -> p c t"),
                                 in1=gA[ic % 2][:, :, tl * 128:(tl + 1) * 128], op=ALU.mult)
                            for half in range(2):
                                po = PS[OPB[tl * 2 + half]]
                                for c in range(IC):
                                    T.op("pe", "matmul", out=po[:, 0:512], lhsT=gam[:, c, :],
                                         rhs=Vc[ic % 2][:, c, half * 512:(half + 1) * 512],
                                         start=(ic == 0 and c == 0), stop=(ic == NIC - 1 and c == IC - 1))
                            T.pe_signal()

                    load_tables(0)
                    emit_at(0)
                    for ic in range(NIC):
                        if ic + 1 < NIC:
                            load_tables(ic + 1)
                            emit_at(ic + 1)
                        emit_gv(ic)
                    for tl in range(TG):
                        ti = g * TG + tl
                        seq, tt = ti // NT, ti % NT
                        T.dma("sp", out=h2f[:], in_=H2s[ti * 128:(ti + 1) * 128, :], _r=["s_h2_%d" % ti])
                        for half in range(2):
                            T.op("dve", "scalar_tensor_tensor", out=rD[:, half * 512:(half + 1) * 512],
                                 in0=h2f[:, half * 512:(half + 1) * 512], scalar=ALPHA,
                                 in1=PS[OPB[tl * 2 + half]][:, 0:512], op0=ALU.mult, op1=ALU.add)
                        yo_ = yo[ti % 2]
                        layer_norm(rD, g3b, b3b, yo_[:], tmpD, stD)
                        T.dma("sp", out=y_out[seq, tt * 128:(tt + 1) * 128, :], in_=yo_[:])
                T.barrier()
        T.finish("sp", ["y_out"])
    return nc


def _prep_inputs(inputs, NSEQ, S, n_cores):
    sq = lambda a: np.ascontiguousarray(np.asarray(a)[0])
    shared = {
        "ln_in_g": np.asarray(inputs["ln_in_g"]), "ln_in_b": np.asarray(inputs["ln_in_b"]),
        "w_in": sq(inputs["w_in"]), "conv_w": sq(inputs["conv_w"]), "conv_b": sq(inputs["conv_b"]),
        "dt_bias": sq(inputs["dt_bias"]), "a_log": sq(inputs["a_log"]), "d_skip": sq(inputs["d_skip"]),
        "ssm_norm_g": sq(inputs["ssm_norm_g"]), "q_norm_g": sq(inputs["q_norm_g"]), "w_q_b": sq(inputs["w_q_b"]),
        "kv_norm_g": sq(inputs["kv_norm_g"]), "w_kv_b": sq(inputs["w_kv_b"]), "w_out": sq(inputs["w_out"]),
        "ln_mix_g": sq(inputs["ln_mix_g"]), "ln_mix_b": sq(inputs["ln_mix_b"]), "w_query": sq(inputs["w_query"]),
        "sub_keys": sq(inputs["sub_keys"]).reshape(16, 128, 128),
        "u_table": sq(inputs["u_table"]), "v_table": sq(inputs["v_table"]),
        "ln_ffn_g": sq(inputs["ln_ffn_g"]), "ln_ffn_b": sq(inputs["ln_ffn_b"]),
    }
    shared = {k: np.ascontiguousarray(v, dtype=np.float32) for k, v in shared.items()}
    shared.update(_consts(S))
    xs = np.asarray(inputs["x"], dtype=np.float32)
    maps = []
    for c in range(n_cores):
        m = dict(shared)
        m["x"] = np.ascontiguousarray(xs[c * NSEQ:(c + 1) * NSEQ])
        maps.append(m)
    return maps


def kernel(**inputs):
    B, S = inputs["x"].shape[0], inputs["x"].shape[1]
    n_cores = 8
    NSEQ = B // n_cores
    nc = build(NSEQ, S)
    maps = _prep_inputs(inputs, NSEQ, S, n_cores)
    res = run_bass_kernel_spmd(nc, maps, core_ids=list(range(n_cores)))
    return np.concatenate([np.asarray(r["y_out"], dtype=np.float32) for r in res.results], axis=0)
```

```python
from contextlib import ExitStack
import numpy as np
import concourse.bass as bass
import concourse.mybir as mybir
from concourse.bass_utils import run_bass_kernel_spmd

F32 = mybir.dt.float32
BF16 = mybir.dt.bfloat16
AF = mybir.ActivationFunctionType
ALU = mybir.AluOpType
AX = mybir.AxisListType

D = 1024
EPS = 1e-5
ALPHA = 2.0 ** 0.25
NEXP = 16384
WRITE_KEYS = ("out", "accum_out", "ap")
NDS = 40
PIPE = True


class Ev:
    __slots__ = ("sem", "val")

    def __init__(self, sem, val):
        self.sem = sem
        self.val = val


class Eng:
    def __init__(self, e, sem, is_pe=False):
        self.name = None
        self.last_idx = None
        self.e = e
        self.sem = sem
        self.cnt = 0
        self.seen = {}
        self.is_pe = is_pe
        self.next_ev = Ev(sem, None)
        self.last_ins = None


class Tracker:
    def __init__(self, nc, stack):
        self.nc = nc
        self.E = {}
        for name, e, pe in (("pe", nc.tensor, True), ("act", nc.scalar, False),
                            ("dve", nc.vector, False), ("pool", nc.gpsimd, False)):
            self.E[name] = Eng(e, stack.enter_context(nc.semaphore("s_" + name)), pe)
        self.E["sp"] = Eng(nc.sync, None)
        for k, v in self.E.items():
            v.name = k
        self.dsem = [stack.enter_context(nc.semaphore("dma%d" % i)) for i in range(NDS)]
        self.bar_sem = stack.enter_context(nc.semaphore("s_bar"))
        self.bar_cnt = 0
        self.duse = [0] * NDS
        self.dnext = 0
        self.W = {}
        self.R = {}
        self.psum_names = set()
        self.log = {k: [] for k in self.E}
        self.semname = {}

    def _keys(self, kw):
        r, w = [], []
        for k, v in kw.items():
            if k in ("_r", "_w"):
                continue
            if hasattr(v, "tensor"):
                name = v.tensor.name
                if k in WRITE_KEYS or name in self.psum_names:
                    w.append(name)
                else:
                    r.append(name)
        r += kw.get("_r", [])
        w += kw.get("_w", [])
        return r, w

    def _flush_pe(self):
        pe = self.E["pe"]
        if pe.last_ins is not None:
            pe.cnt += 1
            pe.last_ins.then_inc(pe.sem, 1)
            self.log["pe"][pe.last_idx][2].append((id(pe.sem), 1))
            pe.next_ev.val = pe.cnt
            pe.next_ev = Ev(pe.sem, None)
            pe.last_ins = None

    def _wait(self, E, rk, wk):
        need = {}

        def add(ev):
            if ev is None:
                return
            if E.is_pe and ev.sem is E.sem:
                return
            if ev.val is None:
                self._flush_pe()
            k = id(ev.sem)
            if k not in need or need[k][1] < ev.val:
                need[k] = (ev.sem, ev.val)

        for k in rk:
            add(self.W.get(k))
        for k in wk:
            add(self.W.get(k))
            for ev in self.R.get(k, {}).values():
                add(ev)
        for k, (sem, val) in need.items():
            if E.seen.get(k, 0) < val:
                E.e.wait_ge(sem, val)
                E.seen[k] = val
                self.log[E.name].append(("w", k, val))

    def _record(self, ev, rk, wk):
        for k in wk:
            self.W[k] = ev
            self.R[k] = {}
        for k in rk:
            if k in wk:
                continue
            self.R.setdefault(k, {})[id(ev.sem)] = ev

    def op(self, eng, fname, **kw):
        E = self.E[eng]
        rk, wk = self._keys(kw)
        self._wait(E, rk, wk)
        args = {k: v for k, v in kw.items() if k not in ("_r", "_w")}
        ins = getattr(E.e, fname)(**args)
        if E.is_pe:
            E.last_ins = ins
            E.last_idx = len(self.log[eng])
            self.log[eng].append(("i", fname, []))
            ev = E.next_ev
        else:
            E.cnt += 1
            ins.then_inc(E.sem, 1)
            self.log[eng].append(("i", fname, [(id(E.sem), 1)]))
            ev = Ev(E.sem, E.cnt)
        self._record(ev, rk, wk)
        return ins

    def dma(self, eng, out, in_, _r=(), _w=(), **kw):
        E = self.E[eng]
        rk = [in_.tensor.name] + list(_r)
        wk = [out.tensor.name] + list(_w)
        self._wait(E, rk, wk)
        s = self.dnext
        self.dnext = (self.dnext + 1) % NDS
        sem = self.dsem[s]
        if self.duse[s] > 0 and E.seen.get(id(sem), 0) < 16 * self.duse[s]:
            E.e.wait_ge(sem, 16 * self.duse[s])
            E.seen[id(sem)] = 16 * self.duse[s]
            self.log[eng].append(("w", id(sem), 16 * self.duse[s]))
        self.duse[s] += 1
        E.e.dma_start(out=out, in_=in_, **kw).then_inc(sem, 16)
        self.log[eng].append(("i", "dma", [(id(sem), 16)]))
        ev = Ev(sem, 16 * self.duse[s])
        self._record(ev, rk, wk)

    def barrier(self):
        self._flush_pe()
        sp = self.E["sp"]
        targets = []
        for name in ("pe", "act", "dve", "pool"):
            E = self.E[name]
            if E.cnt > 0:
                targets.append((E.sem, E.cnt))
        for i in range(NDS):
            if self.duse[i] > 0:
                targets.append((self.dsem[i], 16 * self.duse[i]))
        for sem, val in targets:
            k = id(sem)
            if sp.seen.get(k, 0) < val:
                sp.e.wait_ge(sem, val)
                sp.seen[k] = val
                self.log["sp"].append(("w", k, val))
        self.bar_cnt += 1
        sp.e.sem_inc(self.bar_sem, 1)
        self.log["sp"].append(("i", "sem_inc", [(id(self.bar_sem), 1)]))
        for name in ("pe", "act", "dve", "pool"):
            E = self.E[name]
            E.e.wait_ge(self.bar_sem, self.bar_cnt)
            self.log[name].append(("w", id(self.bar_sem), self.bar_cnt))
            for sem, val in targets:
                E.seen[id(sem)] = max(E.seen.get(id(sem), 0), val)

    def pe_signal(self):
        self._flush_pe()

    def finish(self, eng, keys):
        self._wait(self.E[eng], [], list(keys))
        self.simulate()

    def simulate(self):
        sems = {}
        pc = {k: 0 for k in self.log}
        total = sum(len(v) for v in self.log.values())
        done = 0
        while True:
            progress = False
            for k, lst in self.log.items():
                while pc[k] < len(lst):
                    e = lst[pc[k]]
                    if e[0] == "w":
                        if sems.get(e[1], 0) < e[2]:
                            break
                    else:
                        for sid, amt in e[2]:
                            sems[sid] = sems.get(sid, 0) + amt
                    pc[k] += 1
                    done += 1
                    progress = True
            if done == total:
                return True
            if not progress:
                msg = {k: (pc[k], len(self.log[k]), self.log[k][pc[k]] if pc[k] < len(self.log[k]) else None) for k in self.log}
                raise RuntimeError("semaphore deadlock in host simulation: %r" % (msg,))


def _consts(S):
    ident = np.eye(128, dtype=np.float32)
    triu = np.triu(np.ones((128, 128), np.float32))
    negm = np.where(triu > 0, 0.0, -30000.0).astype(np.float32)
    half = 32
    inv_freq = (10000.0 ** (-np.arange(half, dtype=np.float32) / half)).astype(np.float32)
    ang = (np.arange(S, dtype=np.float32)[:, None] * inv_freq).astype(np.float32)
    cos, sin = np.cos(ang).astype(np.float32), np.sin(ang).astype(np.float32)
    return {
        "c_ident": ident, "c_triu": triu, "c_negm4": np.tile(negm, (1, 4)),
        "c_cos": cos, "c_sin": sin,
        "c_cos2t": np.ascontiguousarray(np.concatenate([cos, cos], 1).T),
        "c_sin2t": np.ascontiguousarray(np.concatenate([sin, sin], 1).T),
    }


def build(NSEQ, S, debug=False, stop_after="all"):
    nc = bass.Bass("TRN2", target_bir_lowering=False)
    NT = S // 128
    NTOK = NSEQ * S
    skind = "ExternalOutput" if debug else "Internal"

    def din(name, shape, dt=F32):
        return nc.dram_tensor(name, list(shape), dt, kind="ExternalInput").ap()

    x = din("x", [NSEQ, S, D])
    ln_in_g, ln_in_b = din("ln_in_g", [D]), din("ln_in_b", [D])
    w_in = din("w_in", [D, 3280])
    conv_w, conv_b = din("conv_w", [4, 1536]), din("conv_b", [1536])
    dt_bias, a_log, d_skip = din("dt_bias", [16]), din("a_log", [16]), din("d_skip", [16])
    ssm_norm_g = din("ssm_norm_g", [D])
    q_norm_g, w_q_b = din("q_norm_g", [384]), din("w_q_b", [384, 1536])
    kv_norm_g, w_kv_b = din("kv_norm_g", [256]), din("w_kv_b", [256, 2048])
    w_out = din("w_out", [2048, D])
    ln_mix_g, ln_mix_b = din("ln_mix_g", [D]), din("ln_mix_b", [D])
    w_query = din("w_query", [D, 2048])
    sub_keys = din("sub_keys", [16, 128, 128])
    u_table, v_table = din("u_table", [NEXP, D]), din("v_table", [NEXP, D])
    ln_ffn_g, ln_ffn_b = din("ln_ffn_g", [D]), din("ln_ffn_b", [D])
    c_ident, c_triu, c_negm4 = din("c_ident", [128, 128]), din("c_triu", [128, 128]), din("c_negm4", [128, 512])
    c_cos, c_sin = din("c_cos", [S, 32]), din("c_sin", [S, 32])
    c_cos2t, c_sin2t = din("c_cos2t", [64, S]), din("c_sin2t", [64, S])

    y_out = nc.dram_tensor("y_out", [NSEQ, S, D], F32, kind="ExternalOutput").ap()
    YS = nc.dram_tensor("s_ys", [NSEQ, 128, 8, S], BF16, kind=skind).ap()
    YM = nc.dram_tensor("s_ym", [NSEQ, 128, 8, S], BF16, kind=skind).ap()
    QLs = nc.dram_tensor("s_ql", [NSEQ, 128, 3, S], BF16, kind=skind).ap()
    KVLs = nc.dram_tensor("s_kvl", [NSEQ, 128, 2, S], BF16, kind=skind).ap()
    KPEs = nc.dram_tensor("s_kpe", [NSEQ, 64, S], BF16, kind=skind).ap()
    UT = nc.dram_tensor("s_ut", [8, 128, NEXP], BF16, kind="Internal").ap()
    VB = nc.dram_tensor("s_vb", [NEXP, D], BF16, kind="Internal").ap()

    with ExitStack() as top:
        T = Tracker(nc, top)

        def sb(name, shape, dt=F32, stack=top):
            return stack.enter_context(nc.sbuf_tensor(name, list(shape), dt))

        PS = []
        for i in range(8):
            t = top.enter_context(nc.psum_tensor("ps%d" % i, [128, 512], F32))
            T.psum_names.add("ps%d" % i)
            PS.append(t)
        pscur = [0]

        def psum():
            t = PS[pscur[0]]
            pscur[0] = (pscur[0] + 1) % 8
            return t

        def psum_from(pool, cur):
            t = PS[pool[cur[0] % len(pool)]]
            cur[0] += 1
            return t

        identf = sb("identf", [128, 128])
        identb = sb("identb", [128, 128], BF16)
        triu = sb("triu", [128, 128])
        ntriu = sb("ntriu", [128, 128])
        onesf = sb("onesf", [128, 128])
        negm4 = sb("negm4", [128, 512])
        T.dma("sp", out=identf[:], in_=c_ident)
        T.dma("sp", out=triu[:], in_=c_triu)
        T.dma("sp", out=negm4[:], in_=c_negm4)
        T.op("dve", "tensor_copy", out=identb[:], in_=identf[:])
        T.op("dve", "tensor_scalar", out=ntriu[:], in0=triu[:], scalar1=-1.0, scalar2=None, op0=ALU.mult)
        T.op("pool", "memset", ap=onesf[:], constant=1.0)

        def bcast_load(name, src, n, stack=top):
            t = sb(name, [128, n], F32, stack)
            T.dma("sp", out=t[:], in_=src.partition_broadcast(128))
            return t

        rr = [0]

        def cast_copy(out, in_):
            e = ("dve", "act", "pool")[rr[0] % 3]
            rr[0] += 1
            if e == "act":
                T.op("act", "copy", out=out, in_=in_)
            else:
                T.op(e, "tensor_copy", out=out, in_=in_)

        def layer_norm(xt, gb, bb, out, tmp, st):
            for h in range(2):
                T.op("dve", "bn_stats", out=st[:, h * 6:(h + 1) * 6], in_=xt[:, h * 512:(h + 1) * 512])
            T.op("dve", "bn_aggr", out=st[:, 12:14], in_=st[:, 0:12])
            T.op("act", "activation", out=st[:, 14:15], in_=st[:, 13:14], func=AF.Sqrt, bias=EPS, scale=1.0)
            T.op("dve", "reciprocal", out=st[:, 15:16], in_=st[:, 14:15])
            T.op("dve", "tensor_scalar", out=tmp[:], in0=xt[:], scalar1=st[:, 12:13], scalar2=st[:, 15:16],
                 op0=ALU.subtract, op1=ALU.mult)
            T.op("pool", "tensor_tensor", out=tmp[:], in0=tmp[:], in1=gb[:], op=ALU.mult)
            T.op("dve", "tensor_tensor", out=out, in0=tmp[:], in1=bb[:], op=ALU.add)

        ln12 = ExitStack()
        g1b = bcast_load("g1b", ln_in_g, D, ln12)
        b1b = bcast_load("b1b", ln_in_b, D, ln12)
        g2b = bcast_load("g2b", ln_mix_g, D, ln12)
        b2b = bcast_load("b2b", ln_mix_b, D, ln12)

        with ExitStack() as pa:
            win = sb("win", [128, 8, 3280], BF16, pa)
            w_in_v = w_in.rearrange("(k p) n -> p k n", p=128)
            with ExitStack() as pws:
                stg = [sb("wstg%d" % i, [128, 8, 410], F32, pws) for i in range(2)]
                for blk in range(8):
                    st_ = stg[blk % 2]
                    T.dma("sp", out=st_[:], in_=w_in_v[:, :, blk * 410:(blk + 1) * 410])
                    cast_copy(win[:, :, blk * 410:(blk + 1) * 410], st_[:])
                T.barrier()
            cw = sb("cw", [128, 12, 4], F32, pa)
            cb = sb("cb", [128, 12], F32, pa)
            for k in range(4):
                T.dma("sp", out=cw[:, :, k:k + 1], in_=conv_w[k].rearrange("(c p o) -> p c o", p=128, o=1),
                      allow_slow_non_contiguous=True)
            T.dma("sp", out=cb[:].unsqueeze(2), in_=conv_b.rearrange("(c p o) -> p c o", p=128, o=1),
                  allow_slow_non_contiguous=True)
            dtb = bcast_load("dtb", dt_bias, 16, pa)
            Ab = bcast_load("Ab", a_log, 16, pa)
            dskb = bcast_load("dskb", d_skip, 16, pa)
            T.op("act", "activation", out=Ab[:], in_=Ab[:], func=AF.Exp)
            T.op("dve", "tensor_scalar", out=Ab[:], in0=Ab[:], scalar1=-1.0, scalar2=None, op0=ALU.mult)
            cosT = sb("cosT", [128, NT, 32], F32, pa)
            sinT = sb("sinT", [128, NT, 32], F32, pa)
            T.dma("sp", out=cosT[:], in_=c_cos.rearrange("(t p) f -> p t f", p=128))
            T.dma("sp", out=sinT[:], in_=c_sin.rearrange("(t p) f -> p t f", p=128))

            xt2 = [sb("xt%d" % i, [128, D], F32, pa) for i in range(3)]
            tmpA = sb("tmpA", [128, D], F32, pa)
            stA = sb("stA", [128, 16], F32, pa)
            hb = sb("hb", [128, D], BF16, pa)
            hT = sb("hT", [128, 8, 128], BF16, pa)
            zs2 = [sb("zs%d" % i, [128, D], F32, pa) for i in range(2)]
            dtv = sb("dtv", [128, 16], F32, pa)
            dtw = sb("dtw", [128, 4, 16], F32, pa)
            dt2 = [sb("dt_t%d" % i, [128, 16], F32, pa) for i in range(2)]
            a2 = [sb("a_t%d" % i, [128, 16], F32, pa) for i in range(2)]
            junk1 = sb("junk1", [128, 640], F32, pa)
            junk2 = sb("junk2", [128, 1024], F32, pa)
            ssq1 = sb("ssq1", [128, 16], F32, pa)
            ssq2 = sb("ssq2", [128, 16], F32, pa)
            latn = sb("latn", [128, 640], BF16, pa)
            kp = sb("kp", [128, 64], F32, pa)
            kpw = sb("kpw", [128, 4, 32], F32, pa)
            kpr = sb("kpr", [128, 64], BF16, pa)
            lat_sb = sb("lat_sb", [128, 768], BF16, pa)
            xraw2 = [sb("xraw%d" % i, [128, 12, 131], F32, pa) for i in range(2)]
            xacc = sb("xacc", [128, 12, 128], F32, pa)
            xtmp = sb("xtmp", [128, 12, 128], F32, pa)
            xact = sb("xact", [128, 12, 128], F32, pa)
            bcb = sb("bcb", [128, 4, 128], BF16, pa)
            xs_tok = sb("xs_tok", [128, D], F32, pa)
            btok = sb("btok", [128, 256], BF16, pa)
            Rm = sb("Rm", [128, 8, 128], F32, pa)
            Abc = sb("Abc", [128, 8, 128], F32, pa)
            expA = sb("expA", [128, 8, 128], F32, pa)
            decT = sb("decT", [128, 8, 128], F32, pa)
            cb_sb = sb("cb_sb", [128, 128], F32, pa)
            MT = sb("MT", [128, 8, 128], BF16, pa)
            CsT = sb("CsT", [128, 8, 128], BF16, pa)
            xdt = sb("xdt", [128, 8, 64], BF16, pa)
            w2 = sb("w2", [128, 8], F32, pa)
            xdtd = sb("xdtd", [128, 8, 64], BF16, pa)
            prevT = sb("prevT", [128, D], F32, pa)
            prevb = sb("prevb", [128, D], BF16, pa)
            ysb = sb("ysb", [128, D], F32, pa)
            ytmp = sb("ytmp", [128, 512], F32, pa)
            ynb = sb("ynb", [128, D], BF16, pa)
            ysT = sb("ysT", [128, 8, 128], BF16, pa)

            tiles = [(seq_, tt_) for seq_ in range(NSEQ) for tt_ in range(NT)]
            P1, P2 = [0, 1], [4, 5, 6, 7]
            c1, c2 = [0], [0]

            def a_load(m):
                sq_, t_ = tiles[m]
                T.dma("sp", out=xt2[m % 3][:], in_=x[sq_, t_ * 128:(t_ + 1) * 128, :])

            def stage1a(n):
                seq, tt = tiles[n]
                zs_, dt_, a_, xr = zs2[n % 2], dt2[n % 2], a2[n % 2], xraw2[n % 2]
                ts = slice(tt * 128, (tt + 1) * 128)
                xt = xt2[n % 3]
                if n == 0:
                    a_load(0)
                    if len(tiles) > 1:
                        a_load(1)
                if n + 2 < len(tiles):
                    a_load(n + 2)
                layer_norm(xt, g1b, b1b, hb[:], tmpA, stA)
                pT = psum_from(P1, c1)
                pTb = pT[:].bitcast(BF16)
                for c in range(8):
                    T.op("pe", "transpose", out=pTb[:, c * 128:(c + 1) * 128], in_=hb[:, c * 128:(c + 1) * 128],
                         identity=identb[:])
                T.op("act", "copy", out=hT[:].rearrange("p c t -> p (c t)"), in_=pTb[:, 0:1024])
                for half in range(2):
                    pz = psum_from(P1, c1)
                    for k in range(8):
                        T.op("pe", "matmul", out=pz[:, 0:512], lhsT=hT[:, k, :],
                             rhs=win[:, k, half * 512:(half + 1) * 512], start=(k == 0), stop=(k == 7))
                    T.op("act", "activation", out=zs_[:, half * 512:(half + 1) * 512], in_=pz[:, 0:512], func=AF.Silu)
                p1 = PS[2]
                for k in range(8):
                    T.op("pe", "matmul", out=p1[:, 0:400], lhsT=hT[:, k, :], rhs=win[:, k, 2560:2960],
                         start=(k == 0), stop=(k == 7))
                p2 = PS[3]
                for k in range(8):
                    T.op("pe", "matmul", out=p2[:, 0:320], lhsT=hT[:, k, :], rhs=win[:, k, 2960:3280],
                         start=(k == 0), stop=(k == 7))
                for b3 in range(3):
                    px = psum_from(P1, c1)
                    for cc in range(4):
                        c = b3 * 4 + cc
                        for k in range(8):
                            T.op("pe", "matmul", out=px[:, cc * 128:(cc + 1) * 128],
                                 lhsT=win[:, k, 1024 + c * 128:1024 + (c + 1) * 128], rhs=hT[:, k, :],
                                 start=(k == 0), stop=(k == 7))
                    T.op("act", "copy", out=xr[:, b3 * 4:(b3 + 1) * 4, 3:131],
                         in_=px[:, 0:512].rearrange("p (c t) -> p c t", c=4))

            def stage1b(n):
                seq, tt = tiles[n]
                ts = slice(tt * 128, (tt + 1) * 128)
                dt_, a_ = dt2[n % 2], a2[n % 2]
                p1, p2 = PS[2], PS[3]
                T.op("dve", "tensor_tensor", out=dtv[:], in0=p1[:, 0:16], in1=dtb[:], op=ALU.add)
                T.op("dve", "tensor_scalar", out=dtw[:, 0, :], in0=dtv[:], scalar1=-1.0, scalar2=None, op0=ALU.mult)
                T.op("dve", "tensor_tensor", out=dtw[:, 3, :], in0=dtv[:], in1=dtw[:, 0, :], op=ALU.min)
                T.op("act", "activation", out=dtw[:, 1, :], in_=dtw[:, 3, :], func=AF.Exp)
                T.op("act", "activation", out=dtw[:, 2, :], in_=dtw[:, 1, :], func=AF.Ln, bias=1.0)
                T.op("dve", "scalar_tensor_tensor", out=dt_[:], in0=dtv[:], scalar=0.0, in1=dtw[:, 2, :],
                     op0=ALU.max, op1=ALU.add)
                T.op("dve", "tensor_tensor", out=a_[:], in0=dt_[:], in1=Ab[:], op=ALU.mult)
                T.op("pool", "memset", ap=ssq1[:], constant=0.0)
                T.op("act", "activation", out=junk1[:, 0:384], in_=p1[:, 16:400], func=AF.Square,
                     accum_out=ssq1[:, 0:1])
                T.op("act", "activation", out=junk1[:, 384:640], in_=p2[:, 0:256], func=AF.Square,
                     accum_out=ssq1[:, 1:2])
                T.op("act", "activation", out=ssq1[:, 2:3], in_=ssq1[:, 0:1], func=AF.Sqrt, bias=EPS, scale=1.0 / 384)
                T.op("act", "activation", out=ssq1[:, 3:4], in_=ssq1[:, 1:2], func=AF.Sqrt, bias=EPS, scale=1.0 / 256)
                T.op("dve", "reciprocal", out=ssq1[:, 4:6], in_=ssq1[:, 2:4])
                T.op("dve", "tensor_scalar", out=latn[:, 0:384], in0=p1[:, 16:400], scalar1=ssq1[:, 4:5],
                     scalar2=None, op0=ALU.mult)
                T.op("dve", "tensor_scalar", out=latn[:, 384:640], in0=p2[:, 0:256], scalar1=ssq1[:, 5:6],
                     scalar2=None, op0=ALU.mult)
                T.op("act", "copy", out=kp[:], in_=p2[:, 256:320])
                cs, sn = cosT[:, tt, :], sinT[:, tt, :]
                T.op("dve", "tensor_tensor", out=kpw[:, 0, :], in0=kp[:, 0:32], in1=cs, op=ALU.mult)
                T.op("dve", "tensor_tensor", out=kpw[:, 1, :], in0=kp[:, 32:64], in1=sn, op=ALU.mult)
                T.op("dve", "tensor_tensor", out=kpw[:, 2, :], in0=kp[:, 32:64], in1=cs, op=ALU.mult)
                T.op("dve", "tensor_tensor", out=kpw[:, 3, :], in0=kp[:, 0:32], in1=sn, op=ALU.mult)
                T.op("dve", "tensor_tensor", out=kpr[:, 0:32], in0=kpw[:, 0, :], in1=kpw[:, 1, :], op=ALU.subtract)
                T.op("dve", "tensor_tensor", out=kpr[:, 32:64], in0=kpw[:, 2, :], in1=kpw[:, 3, :], op=ALU.add)
                pL = psum_from(P1, c1)
                pLb = pL[:].bitcast(BF16)
                for c in range(5):
                    T.op("pe", "transpose", out=pLb[:, c * 128:(c + 1) * 128], in_=latn[:, c * 128:(c + 1) * 128],
                         identity=identb[:])
                T.op("pe", "transpose", out=pLb[0:64, 640:768], in_=kpr[:, 0:64], identity=identb[:])
                T.op("dve", "tensor_copy", out=lat_sb[:, 0:640], in_=pLb[:, 0:640])
                T.op("dve", "tensor_copy", out=lat_sb[0:64, 640:768], in_=pLb[0:64, 640:768])
                T.dma("sp", out=QLs[seq, :, :, ts], in_=lat_sb[:, 0:384].rearrange("p (c t) -> p c t", c=3))
                T.dma("sp", out=KVLs[seq, :, :, ts], in_=lat_sb[:, 384:640].rearrange("p (c t) -> p c t", c=2))
                T.dma("sp", out=KPEs[seq, :, ts], in_=lat_sb[0:64, 640:768])

            def stage2a(n):
                seq, tt = tiles[n]
                ts = slice(tt * 128, (tt + 1) * 128)
                zs_, dt_, a_, xr = zs2[n % 2], dt2[n % 2], a2[n % 2], xraw2[n % 2]
                if tt == 0:
                    T.op("pool", "memset", ap=xr[:, :, 0:3], constant=0.0)
                    T.op("pool", "memset", ap=prevT[:], constant=0.0)
                    T.op("pool", "memset", ap=prevb[:], constant=0.0)
                T.op("pool", "memset", ap=ssq2[:], constant=0.0)
                T.op("dve", "tensor_tensor", out=xacc[:], in0=xr[:, :, 0:128],
                     in1=cw[:, :, 0:1].to_broadcast([128, 12, 128]), op=ALU.mult)
                for k in range(1, 4):
                    T.op("pool", "tensor_tensor", out=xtmp[:], in0=xr[:, :, k:k + 128],
                         in1=cw[:, :, k:k + 1].to_broadcast([128, 12, 128]), op=ALU.mult)
                    T.op("dve", "tensor_tensor", out=xacc[:], in0=xacc[:], in1=xtmp[:], op=ALU.add)
                T.op("pool", "tensor_tensor", out=xacc[:], in0=xacc[:],
                     in1=cb[:].unsqueeze(2).to_broadcast([128, 12, 128]), op=ALU.add)
                T.op("act", "activation", out=xact[:], in_=xacc[:], func=AF.Silu)
                if tt + 1 < NT:
                    T.op("pool", "tensor_copy", out=xraw2[(n + 1) % 2][:, :, 0:3], in_=xr[:, :, 128:131])
                T.op("dve", "tensor_copy", out=bcb[:], in_=xact[:, 8:12, :])
                for b2 in range(2):
                    pt = psum_from(P2, c2)
                    for cc in range(4):
                        T.op("pe", "transpose", out=pt[:, cc * 128:(cc + 1) * 128], in_=xact[:, b2 * 4 + cc, :],
                             identity=identf[:])
                    T.op("act", "copy", out=xs_tok[:, b2 * 512:(b2 + 1) * 512], in_=pt[:, 0:512])
                pt = psum_from(P2, c2)
                for cc in range(2):
                    T.op("pe", "transpose", out=pt[:, cc * 128:(cc + 1) * 128], in_=xact[:, 8 + cc, :],
                         identity=identf[:])
                T.op("act", "copy", out=btok[:], in_=pt[:, 0:256])

            def stage2b(n):
                seq, tt = tiles[n]
                ts = slice(tt * 128, (tt + 1) * 128)
                zs_, dt_, a_, xr = zs2[n % 2], dt2[n % 2], a2[n % 2], xraw2[n % 2]
                for g in range(2):
                    hs = slice(g * 8, (g + 1) * 8)
                    fs = slice(g * 512, (g + 1) * 512)
                    a_g = a_[:, hs].unsqueeze(2).to_broadcast([128, 8, 128])
                    T.op("dve", "tensor_tensor", out=Rm[:], in0=a_g,
                         in1=triu[:].unsqueeze(1).to_broadcast([128, 8, 128]), op=ALU.mult)
                    T.op("pool", "tensor_copy", out=Abc[:], in_=a_g)
                    for j in range(2):
                        pa_ = psum_from(P2, c2)
                        T.op("pe", "matmul", out=pa_[:, 0:512], lhsT=onesf[:],
                             rhs=Rm[:, 4 * j:4 * j + 4, :].rearrange("p a b -> p (a b)"), start=True, stop=True)
                        T.op("act", "activation", out=expA[:, 4 * j:4 * j + 4, :].rearrange("p a b -> p (a b)"),
                             in_=pa_[:, 0:512], func=AF.Exp)
                        pe_ = psum_from(P2, c2)
                        T.op("pe", "matmul", out=pe_[:, 0:512], lhsT=onesf[:],
                             rhs=Rm[:, 4 * j:4 * j + 4, :].rearrange("p a b -> p (a b)"), start=True, stop=False)
                        T.op("pe", "matmul", out=pe_[:, 0:512], lhsT=ntriu[:],
                             rhs=Abc[:, 4 * j:4 * j + 4, :].rearrange("p a b -> p (a b)"), start=False, stop=False)
                        T.op("pe", "matmul", out=pe_[:, 0:512], lhsT=identf[:], rhs=negm4[:], start=False, stop=True)
                        T.op("act", "activation", out=decT[:, 4 * j:4 * j + 4, :].rearrange("p a b -> p (a b)"),
                             in_=pe_[:, 0:512], func=AF.Exp)
                    pc = psum_from(P2, c2)
                    T.op("pe", "matmul", out=pc[:, 0:128], lhsT=bcb[:, g, :], rhs=bcb[:, 2 + g, :], start=True, stop=True)
                    T.op("act", "copy", out=cb_sb[:], in_=pc[:, 0:128])
                    T.op("dve", "tensor_tensor", out=MT[:], in0=decT[:],
                         in1=cb_sb[:].unsqueeze(1).to_broadcast([128, 8, 128]), op=ALU.mult)
                    T.op("pool", "tensor_tensor", out=CsT[:], in0=expA[:],
                         in1=xact[:, 10 + g, :].unsqueeze(1).to_broadcast([128, 8, 128]), op=ALU.mult)
                    xs_g = xs_tok[:, fs].rearrange("p (h d) -> p h d", h=8)
                    T.op("dve", "tensor_tensor", out=xdt[:], in0=xs_g,
                         in1=dt_[:, hs].unsqueeze(2).to_broadcast([128, 8, 64]), op=ALU.mult)
                    T.op("dve", "tensor_tensor", out=w2[:], in0=dt_[:, hs], in1=decT[:, :, 127], op=ALU.mult)
                    T.op("pool", "tensor_tensor", out=xdtd[:], in0=xs_g,
                         in1=w2[:].unsqueeze(2).to_broadcast([128, 8, 64]), op=ALU.mult)
                    py = psum_from(P2, c2)
                    for j in range(8):
                        T.op("pe", "matmul", out=py[:, j * 64:(j + 1) * 64], lhsT=MT[:, j, :], rhs=xdt[:, j, :],
                             start=True, stop=False)
                        T.op("pe", "matmul", out=py[:, j * 64:(j + 1) * 64], lhsT=CsT[:, j, :],
                             rhs=prevb[:, g * 512 + j * 64:g * 512 + (j + 1) * 64], start=False, stop=True)
                    pst = psum_from(P2, c2)
                    for j in range(8):
                        T.op("pe", "matmul", out=pst[:, j * 64:(j + 1) * 64], lhsT=btok[:, g * 128:(g + 1) * 128],
                             rhs=xdtd[:, j, :], start=True, stop=True)
                    T.op("pool", "tensor_tensor", out=ytmp[:].rearrange("p (h d) -> p h d", h=8), in0=xs_g,
                         in1=dskb[:, hs].unsqueeze(2).to_broadcast([128, 8, 64]), op=ALU.mult)
                    T.op("dve", "tensor_tensor", out=ysb[:, fs], in0=ytmp[:], in1=py[:, 0:512], op=ALU.add)
                    pv_g = prevT[:, fs].rearrange("p (h d) -> p h d", h=8)
                    T.op("dve", "tensor_tensor", out=pv_g, in0=pv_g,
                         in1=expA[:, :, 127:128].to_broadcast([128, 8, 64]), op=ALU.mult)
                    T.op("dve", "tensor_tensor", out=prevT[:, fs], in0=prevT[:, fs], in1=pst[:, 0:512], op=ALU.add)
                    T.op("act", "copy", out=prevb[:, fs], in_=prevT[:, fs])
                T.op("dve", "tensor_tensor", out=ysb[:], in0=ysb[:], in1=zs_[:], op=ALU.mult)
                T.op("act", "activation", out=junk2[:], in_=ysb[:], func=AF.Square, accum_out=ssq2[:, 6:7])
                T.op("act", "activation", out=ssq2[:, 7:8], in_=ssq2[:, 6:7], func=AF.Sqrt, bias=EPS, scale=1.0 / 1024)
                T.op("dve", "reciprocal", out=ssq2[:, 8:9], in_=ssq2[:, 7:8])
                T.op("dve", "tensor_scalar", out=ynb[:], in0=ysb[:], scalar1=ssq2[:, 8:9], scalar2=None,
                     op0=ALU.mult)
                pY = psum_from(P2, c2)
                pYb = pY[:].bitcast(BF16)
                for c in range(8):
                    T.op("pe", "transpose", out=pYb[:, c * 128:(c + 1) * 128], in_=ynb[:, c * 128:(c + 1) * 128],
                         identity=identb[:])
                T.op("act", "copy", out=ysT[:].rearrange("p c t -> p (c t)"), in_=pYb[:, 0:1024])
                T.dma("sp", out=YS[seq, :, :, ts], in_=ysT[:])

            stage1a(0)
            stage1b(0)
            for n in range(len(tiles)):
                if n + 1 < len(tiles):
                    stage1a(n + 1)
                stage2a(n)
                if n + 1 < len(tiles):
                    stage1b(n + 1)
                stage2b(n)


        if stop_after == "A":
            T.finish("sp", ["s_ys", "s_ql", "s_kvl", "s_kpe"])
            ln12.close()
            return nc

        SCALE = 192.0 ** -0.5
        NQG = S // 512
        with ExitStack() as pb:
            wqs = sb("wqs", [128, 3, 1536], F32, pb)
            wq = sb("wq", [128, 3, 1536], BF16, pb)
            wqr = sb("wqr", [128, 3, 8, 64], BF16, pb)
            gq = sb("gq", [128, 3], F32, pb)
            gkv = sb("gkv", [128, 2], F32, pb)
            wkv = sb("wkv", [128, 2, 2048], BF16, pb)
            T.dma("sp", out=wqs[:], in_=w_q_b.rearrange("(c p) n -> p c n", p=128))
            T.dma("sp", out=gq[:].unsqueeze(2), in_=q_norm_g.rearrange("(c p o) -> p c o", p=128, o=1),
                  allow_slow_non_contiguous=True)
            T.dma("sp", out=gkv[:].unsqueeze(2), in_=kv_norm_g.rearrange("(c p o) -> p c o", p=128, o=1),
                  allow_slow_non_contiguous=True)
            for c in range(3):
                T.op("dve", "tensor_scalar", out=wq[:, c, :], in0=wqs[:, c, :], scalar1=gq[:, c:c + 1], scalar2=None,
                     op0=ALU.mult)
            wq4 = wq[:].rearrange("p c (h d) -> p c h d", h=8)
            for c in range(3):
                T.op("dve", "tensor_scalar", out=wqr[:, c, :, 0:32], in0=wq4[:, c, :, 160:192], scalar1=-1.0, scalar2=None,
                     op0=ALU.mult)
                T.op("pool", "tensor_copy", out=wqr[:, c, :, 32:64], in_=wq4[:, c, :, 128:160])
            for c in range(2):
                T.dma("sp", out=wqs[:, 0, :], in_=w_kv_b[c * 128:(c + 1) * 128, 0:1536])
                T.dma("sp", out=wqs[:, 1, 0:512], in_=w_kv_b[c * 128:(c + 1) * 128, 1536:2048])
                T.op("dve", "tensor_scalar", out=wkv[:, c, 0:1536], in0=wqs[:, 0, :], scalar1=gkv[:, c:c + 1],
                     scalar2=None, op0=ALU.mult)
                T.op("dve", "tensor_scalar", out=wkv[:, c, 1536:2048], in0=wqs[:, 1, 0:512], scalar1=gkv[:, c:c + 1],
                     scalar2=None, op0=ALU.mult)
            cos2 = sb("cos2", [64, S], F32, pb)
            sin2 = sb("sin2", [64, S], F32, pb)
            T.dma("sp", out=cos2[:], in_=c_cos2t)
            T.dma("sp", out=sin2[:], in_=c_sin2t)
            triub = sb("triub", [128, 128], BF16, pb)
            T.op("dve", "tensor_copy", out=triub[:], in_=triu[:])
            QL = sb("QL", [128, 3, S], BF16, pb)
            KVL = sb("KVL", [128, 2, S], BF16, pb)
            KPE = sb("KPE", [128, S], BF16, pb)
            QN = sb("QN", [128, S], BF16, pb)
            QP = sb("QP", [128, S], BF16, pb)
            T.op("pool", "memset", ap=KPE[64:128, :], constant=0.0)
            T.op("pool", "memset", ap=QP[64:128, :], constant=0.0)
            KN = sb("KN", [128, S], BF16, pb)
            Vt = sb("Vt", [128, NT, 129], BF16, pb)
            rt1 = sb("rt1", [64, 512], F32, pb)
            rt2 = sb("rt2", [64, 512], F32, pb)
            ptb = [sb("ptb%d" % i, [128, 512], BF16, pb) for i in range(3)]
            rcp = sb("rcp", [128, 4], F32, pb)
            ob = sb("ob", [128, 128], BF16, pb)
            ymT = [sb("ymT%d" % i, [128, 512], BF16, pb) for i in range(2)]
            T.op("pool", "memset", ap=Vt[:, :, 128:129], constant=1.0)
            OB = [0, 1, 2, 3]
            SP_ = [4, 5, 6, 7]
            scur = [0]
            pcount = 0
            for seq in range(NSEQ):
                T.dma("sp", out=QL[:], in_=QLs[seq])
                T.dma("sp", out=KVL[:], in_=KVLs[seq])
                T.dma("sp", out=KPE[0:64, :], in_=KPEs[seq])
                for h in range(8):
                    qc = h * 192
                    kc = h * 256
                    for blk in range(NQG):
                        bs = slice(blk * 512, (blk + 1) * 512)
                        p = psum_from(SP_, scur)
                        for c in range(3):
                            T.op("pe", "matmul", out=p[:, 0:512], lhsT=wq[:, c, qc:qc + 128], rhs=QL[:, c, bs],
                                 start=(c == 0), stop=(c == 2))
                        T.op("act", "copy", out=QN[:, bs], in_=p[:, 0:512])
                        p = psum_from(SP_, scur)
                        for c in range(3):
                            T.op("pe", "matmul", out=p[0:64, 0:512], lhsT=wq[:, c, qc + 128:qc + 192], rhs=QL[:, c, bs],
                                 start=(c == 0), stop=(c == 2))
                        T.op("dve", "tensor_tensor", out=rt1[:], in0=p[0:64, 0:512], in1=cos2[:, bs], op=ALU.mult)
                        p = psum_from(SP_, scur)
                        for c in range(3):
                            T.op("pe", "matmul", out=p[0:64, 0:512], lhsT=wqr[:, c, h, :], rhs=QL[:, c, bs],
                                 start=(c == 0), stop=(c == 2))
                        T.op("dve", "tensor_tensor", out=rt2[:], in0=p[0:64, 0:512], in1=sin2[:, bs], op=ALU.mult)
                        T.op("pool", "tensor_tensor", out=QP[0:64, bs], in0=rt1[:], in1=rt2[:], op=ALU.add)
                        p = psum_from(SP_, scur)
                        for c in range(2):
                            T.op("pe", "matmul", out=p[:, 0:512], lhsT=wkv[:, c, kc:kc + 128], rhs=KVL[:, c, bs],
                                 start=(c == 0), stop=(c == 1))
                        T.op("dve", "tensor_copy", out=KN[:, bs], in_=p[:, 0:512])
                        p = psum_from(SP_, scur)
                        for t4 in range(4):
                            tsl = slice(blk * 512 + t4 * 128, blk * 512 + (t4 + 1) * 128)
                            for c in range(2):
                                T.op("pe", "matmul", out=p[:, t4 * 128:(t4 + 1) * 128], lhsT=KVL[:, c, tsl],
                                     rhs=wkv[:, c, kc + 128:kc + 256], start=(c == 0), stop=(c == 1))
                        T.op("act", "copy", out=Vt[:, blk * 4:(blk + 1) * 4, 0:128],
                             in_=p[:, 0:512].rearrange("p (t d) -> p t d", t=4))
                    for qg in range(NQG):
                        qs = slice(qg * 512, (qg + 1) * 512)
                        nkt = 4 * qg + 4
                        def s_mm(kt_):
                            ks_ = slice(kt_ * 128, (kt_ + 1) * 128)
                            p_ = psum_from(SP_, scur)
                            T.op("pe", "matmul", out=p_[:, 0:512], lhsT=KN[:, ks_], rhs=QN[:, qs], start=True, stop=False)
                            T.op("pe", "matmul", out=p_[:, 0:512], lhsT=KPE[:, ks_], rhs=QP[:, qs], start=False, stop=True)
                            T.pe_signal()
                            return p_
                        p_next = s_mm(0)
                        for kt in range(nkt):
                            p = p_next
                            if kt + 1 < nkt:
                                p_next = s_mm(kt + 1)
                            jmin = max(0, kt - 4 * qg)
                            pt = ptb[pcount % 3]
                            pcount += 1
                            T.op("act", "activation", out=pt[:, jmin * 128:512], in_=p[:, jmin * 128:512], func=AF.Exp,
                                 scale=SCALE)
                            if kt >= 4 * qg:
                                T.op("pool", "tensor_tensor", out=pt[:, jmin * 128:(jmin + 1) * 128],
                                     in0=pt[:, jmin * 128:(jmin + 1) * 128], in1=triub[:], op=ALU.mult)
                            for j in range(jmin, 4):
                                T.op("pe", "matmul", out=PS[OB[j]][:, 0:129], lhsT=pt[:, j * 128:(j + 1) * 128],
                                     rhs=Vt[:, kt, :], start=(kt == 0), stop=(kt == 4 * qg + j))
                            T.pe_signal()
                        ym = ymT[qg % 2]
                        for j in range(4):
                            T.op("dve", "reciprocal", out=rcp[:, j:j + 1], in_=PS[OB[j]][:, 128:129])
                            T.op("dve", "tensor_scalar", out=ob[:], in0=PS[OB[j]][:, 0:128], scalar1=rcp[:, j:j + 1],
                                 scalar2=None, op0=ALU.mult)
                            p = psum_from(SP_, scur)
                            pbf = p[:].bitcast(BF16)
                            T.op("pe", "transpose", out=pbf[:, 0:128], in_=ob[:], identity=identb[:])
                            T.op("act", "copy", out=ym[:, j * 128:(j + 1) * 128], in_=pbf[:, 0:128])
                        T.dma("sp", out=YM[seq, :, h, qs], in_=ym[:])

        if stop_after == "B":
            T.finish("sp", ["s_ys", "s_ql", "s_kvl", "s_kpe", "s_ym"])
            ln12.close()
            return nc

        H2s = nc.dram_tensor("s_h2", [NTOK, D], F32, kind=skind).ap()
        H2Ts = nc.dram_tensor("s_h2t", [128, 8, NTOK], BF16, kind="Internal").ap()
        SCs = nc.dram_tensor("s_sc", [NTOK // 128, 128, 16, 128], F32, kind="Internal").ap()
        SMs = nc.dram_tensor("s_sm", [NTOK // 128, 128, 8, 33], F32, kind="Internal").ap()

        with ExitStack() as pt_:
            NBUF = 3
            tstV = [sb("tstV%d" % i, [128, 4, D], F32, pt_) for i in range(NBUF)]
            tstU = [sb("tstU%d" % i, [128, 4, D], F32, pt_) for i in range(NBUF)]
            tbfV = [sb("tbfV%d" % i, [128, 4, D], BF16, pt_) for i in range(2)]
            tbfU = [sb("tbfU%d" % i, [128, 4, D], BF16, pt_) for i in range(2)]
            utile = [sb("utile%d" % i, [128, 8, 512], BF16, pt_) for i in range(2)]
            u_v = u_table.rearrange("(n c p) d -> n p c d", c=4, p=128)
            v_v = v_table.rearrange("(n c p) d -> n p c d", c=4, p=128)
            vb_v = VB.rearrange("(n c p) d -> n p c d", c=4, p=128)
            ut_v = UT.rearrange("k p e -> p k e")
            NPR = NEXP // 512

            def prep_load(n):
                T.dma("sp", out=tstV[n % NBUF][:], in_=v_v[n])
                T.dma("sp", out=tstU[n % NBUF][:], in_=u_v[n])

            prep_load(0)
            prep_load(1)
            for n in range(NPR):
                if n + 2 < NPR:
                    prep_load(n + 2)
                bfv, bfu = tbfV[n % 2], tbfU[n % 2]
                cast_copy(bfv[:], tstV[n % NBUF][:])
                T.dma("sp", out=vb_v[n], in_=bfv[:], _w=["s_vb%d" % n])
                cast_copy(bfu[:], tstU[n % NBUF][:])
                ut_ = utile[n % 2]
                for c4 in range(4):
                    p = psum()
                    pbf = p[:].bitcast(BF16)
                    for k in range(8):
                        T.op("pe", "transpose", out=pbf[:, k * 128:(k + 1) * 128], in_=bfu[:, c4, k * 128:(k + 1) * 128],
                             identity=identb[:])
                    T.pe_signal()
                    if c4 % 2 == 0:
                        T.op("act", "copy", out=ut_[:, :, c4 * 128:(c4 + 1) * 128],
                             in_=pbf[:, 0:1024].rearrange("p (k e) -> p k e", k=8))
                    else:
                        T.op("dve", "tensor_copy", out=ut_[:, :, c4 * 128:(c4 + 1) * 128],
                             in_=pbf[:, 0:1024].rearrange("p (k e) -> p k e", k=8))
                T.dma("sp", out=ut_v[:, :, n * 512:(n + 1) * 512], in_=ut_[:], _w=["s_ut%d" % n])

        with ExitStack() as pc_:
            wout = sb("wout", [128, 16, D], BF16, pc_)
            gs = sb("gs", [128, 8], F32, pc_)
            T.dma("sp", out=gs[:].unsqueeze(2), in_=ssm_norm_g.rearrange("(c p o) -> p c o", p=128, o=1),
                  allow_slow_non_contiguous=True)
            wst = [sb("wst%d" % i, [128, 2, D], F32, pc_) for i in range(2)]
            wo_v = w_out.rearrange("(k p) n -> p k n", p=128)
            for b in range(8):
                st_ = wst[b % 2]
                T.dma("sp", out=st_[:], in_=wo_v[:, 2 * b:2 * b + 2, :])
                for kk in range(2):
                    k = 2 * b + kk
                    if k < 8:
                        T.op("dve", "tensor_scalar", out=wout[:, k, :], in0=st_[:, kk, :], scalar1=gs[:, k:k + 1],
                             scalar2=None, op0=ALU.mult)
                    else:
                        cast_copy(wout[:, k, :], st_[:, kk, :])
            wqy = sb("wqy", [128, 8, 2048], BF16, pc_)
            KT = sb("KT", [128, 16, 128], BF16, pc_)
            with ExitStack() as pw:
                qst = [sb("qst%d" % i, [128, 8, 256], F32, pw) for i in range(2)]
                wq_v = w_query.rearrange("(k p) n -> p k n", p=128)
                for b in range(8):
                    st_ = qst[b % 2]
                    T.dma("sp", out=st_[:], in_=wq_v[:, :, b * 256:(b + 1) * 256])
                    cast_copy(wqy[:, :, b * 256:(b + 1) * 256], st_[:])
                skf = sb("skf", [128, 16, 128], F32, pw)
                skb = sb("skb", [128, 16, 128], BF16, pw)
                T.dma("sp", out=skf[:], in_=sub_keys.rearrange("k n d -> n k d"))
                T.op("dve", "tensor_copy", out=skb[:], in_=skf[:])
                for b in range(2):
                    p = psum()
                    pbf = p[:].bitcast(BF16)
                    for j in range(8):
                        T.op("pe", "transpose", out=pbf[:, j * 128:(j + 1) * 128], in_=skb[:, b * 8 + j, :],
                             identity=identb[:])
                    T.op("act", "copy", out=KT[:, b * 8:(b + 1) * 8, :].rearrange("p a b -> p (a b)"), in_=pbf[:, 0:1024])
                T.barrier()
            qT = sb("qT", [128, 16, 128], BF16, pc_)
            SCt2 = [sb("SCt%d" % i, [128, 16, 128], F32, pc_) for i in range(2)]
            SMt2 = [sb("SMt%d" % i, [128, 8, 33], F32, pc_) for i in range(2)]
            scwL = [sb("scw%d" % i, [128, 128], F32, pc_) for i in range(16)]
            svA = [sb("svA%d" % i, [128, 8], F32, pc_) for i in range(16)]
            svB = [sb("svB%d" % i, [128, 8], F32, pc_) for i in range(16)]
            cwL = [sb("cw%d" % i, [128, 256], F32, pc_) for i in range(8)]
            tpA = [sb("tpA%d" % i, [128, 8], F32, pc_) for i in range(8)]
            tpB = [sb("tpB%d" % i, [128, 8], F32, pc_) for i in range(8)]
            sv = sb("sv", [128, 16, 16], F32, pc_)
            cand = sb("cand", [128, 8, 256], F32, pc_)
            top = sb("top", [128, 8, 16], F32, pc_)
            exz = sb("exz", [128, 8, 16], F32, pc_)
            zz = sb("zz", [128, 16], F32, pc_)
            xc = [sb("xc%d" % i, [128, D], F32, pc_) for i in range(2)]
            yst = [sb("yst%d" % i, [128, 8, 128], BF16, pc_) for i in range(2)]
            ymt = [sb("ymt%d" % i, [128, 8, 128], BF16, pc_) for i in range(2)]
            tmpC = sb("tmpC", [128, D], F32, pc_)
            stC = sb("stC", [128, 16], F32, pc_)
            hC = sb("hC", [128, D], F32, pc_)
            rC = sb("rC", [128, D], F32, pc_)
            h2 = [sb("h2_%d" % i, [128, D], F32, pc_) for i in range(2)]
            h2b = sb("h2b", [128, D], BF16, pc_)
            h2T = [sb("h2T%d" % i, [128, 8, 128], BF16, pc_) for i in range(2)]
            def c_loads(ti_):
                seq_, tt_ = ti_ // NT, ti_ % NT
                ts_ = slice(tt_ * 128, (tt_ + 1) * 128)
                T.dma("sp", out=xc[ti_ % 2][:], in_=x[seq_, ts_, :])
                T.dma("sp", out=yst[ti_ % 2][:], in_=YS[seq_, :, :, ts_])
                T.dma("sp", out=ymt[ti_ % 2][:], in_=YM[seq_, :, :, ts_])

            for seq in range(NSEQ):
                for tt in range(NT):
                    ti = seq * NT + tt
                    ts = slice(tt * 128, (tt + 1) * 128)
                    gsl = slice(ti * 128, (ti + 1) * 128)
                    xt, ys_, ym_ = xc[ti % 2], yst[ti % 2], ymt[ti % 2]
                    if ti == 0:
                        c_loads(0)
                    if ti + 1 < NSEQ * NT:
                        c_loads(ti + 1)
                    layer_norm(xt, g1b, b1b, hC[:], tmpC, stC)
                    for half in range(2):
                        p = psum()
                        for k in range(16):
                            lhs = ys_[:, k, :] if k < 8 else ym_[:, k - 8, :]
                            T.op("pe", "matmul", out=p[:, 0:512], lhsT=lhs, rhs=wout[:, k, half * 512:(half + 1) * 512],
                                 start=(k == 0), stop=(k == 15))
                        T.op("dve", "scalar_tensor_tensor", out=rC[:, half * 512:(half + 1) * 512],
                             in0=hC[:, half * 512:(half + 1) * 512], scalar=ALPHA, in1=p[:, 0:512],
                             op0=ALU.mult, op1=ALU.add)
                    h2_ = h2[ti % 2]
                    layer_norm(rC, g2b, b2b, h2_[:], tmpC, stC)
                    T.dma("sp", out=H2s[gsl, :], in_=h2_[:], _w=["s_h2_%d" % ti])
                    T.op("act", "copy", out=h2b[:], in_=h2_[:])
                    p = psum()
                    pbf = p[:].bitcast(BF16)
                    for c in range(8):
                        T.op("pe", "transpose", out=pbf[:, c * 128:(c + 1) * 128], in_=h2b[:, c * 128:(c + 1) * 128],
                             identity=identb[:])
                    h2T_ = h2T[ti % 2]
                    T.op("dve", "tensor_copy", out=h2T_[:].rearrange("p c t -> p (c t)"), in_=pbf[:, 0:1024])
                    T.dma("sp", out=H2Ts[:, :, gsl], in_=h2T_[:], _w=["s_h2t_%d" % ti])
                    SCt, SMt = SCt2[ti % 2], SMt2[ti % 2]
                    for b in range(4):
                        p = psum()
                        for j in range(4):
                            hk = b * 4 + j
                            for k in range(8):
                                T.op("pe", "matmul", out=p[:, j * 128:(j + 1) * 128], lhsT=wqy[:, k, hk * 128:(hk + 1) * 128],
                                     rhs=h2T_[:, k, :], start=(k == 0), stop=(k == 7))
                        T.op("act", "copy", out=qT[:, b * 4:(b + 1) * 4, :].rearrange("p a b -> p (a b)"), in_=p[:, 0:512])
                    for b in range(4):
                        p = psum()
                        for j in range(4):
                            hk = b * 4 + j
                            T.op("pe", "matmul", out=p[:, j * 128:(j + 1) * 128], lhsT=qT[:, hk, :], rhs=KT[:, hk, :],
                                 start=True, stop=True)
                        T.op("act", "copy", out=SCt[:, b * 4:(b + 1) * 4, :].rearrange("p a b -> p (a b)"), in_=p[:, 0:512])
                    T.dma("sp", out=SCs[ti], in_=SCt[:], _w=["s_sc_%d" % ti])
                    for hk in range(16):
                        T.op("dve", "max", out=svA[hk][:], in_=SCt[:, hk, :], _r=["SCt%d_%d" % (ti % 2, hk)])
                    for hk in range(16):
                        T.op("dve", "match_replace", out=scwL[hk][:], in_to_replace=svA[hk][:], in_values=SCt[:, hk, :],
                             imm_value=-1e30)
                    for hk in range(16):
                        T.op("dve", "max", out=svB[hk][:], in_=scwL[hk][:])
                    for hk in range(16):
                        T.op("pool", "tensor_copy", out=sv[:, hk, 0:8], in_=svA[hk][:])
                        T.op("pool", "tensor_copy", out=sv[:, hk, 8:16], in_=svB[hk][:])
                    sv4 = sv[:].rearrange("p (h k) a -> p h k a", k=2)
                    T.op("pool", "tensor_tensor", out=cand[:].rearrange("p h (a b) -> p h a b", a=16),
                         in0=sv4[:, :, 0, :].unsqueeze(3).to_broadcast([128, 8, 16, 16]),
                         in1=sv4[:, :, 1, :].unsqueeze(2).to_broadcast([128, 8, 16, 16]), op=ALU.add)
                    for h in range(8):
                        T.op("dve", "max", out=tpA[h][:], in_=cand[:, h, :])
                    for h in range(8):
                        T.op("dve", "match_replace", out=cwL[h][:], in_to_replace=tpA[h][:], in_values=cand[:, h, :],
                             imm_value=-1e30)
                    for h in range(8):
                        T.op("dve", "max", out=tpB[h][:], in_=cwL[h][:])
                    for h in range(8):
                        T.op("pool", "tensor_copy", out=top[:, h, 0:8], in_=tpA[h][:])
                        T.op("pool", "tensor_copy", out=top[:, h, 8:16], in_=tpB[h][:])
                    T.op("dve", "tensor_tensor", out=SMt[:, :, 0:16], in0=top[:, :, 15:16].to_broadcast([128, 8, 16]),
                         in1=sv4[:, :, 0, :], op=ALU.subtract)
                    T.op("pool", "tensor_copy", out=SMt[:, :, 16:32], in_=sv4[:, :, 0, :])
                    T.op("dve", "tensor_tensor", out=exz[:], in0=top[:],
                         in1=top[:, :, 15:16].to_broadcast([128, 8, 16]), op=ALU.subtract)
                    T.op("act", "activation", out=exz[:], in_=exz[:], func=AF.Exp)
                    T.op("dve", "tensor_reduce", out=zz[:, 0:8], in_=exz[:], axis=AX.X, op=ALU.add)
                    T.op("dve", "reciprocal", out=SMt[:, :, 32], in_=zz[:, 0:8])
                    T.dma("sp", out=SMs[ti], in_=SMt[:], _w=["s_sm_%d" % ti])

        if stop_after == "C":
            T.finish("sp", ["s_h2", "s_h2t", "s_ut", "s_vb", "s_sc", "s_sm"] + ["s_h2_%d" % i for i in range(NSEQ * NT)])
            ln12.close()
            return nc

        ln12.close()
        TG = 2
        IC = 4
        NG = NTOK // (TG * 128)
        NIC = 128 // IC
        g3b = bcast_load("g3b", ln_ffn_g, D)
        b3b = bcast_load("b3b", ln_ffn_b, D)
        ut_v = UT.rearrange("k p e -> p k e")
        vb_v2 = VB.rearrange("(n c p) d -> n p c d", c=IC, p=128)
        OPB = [0, 1, 2, 3]
        CP = [4, 5, 6, 7]
        ATP = [4, 5]
        with ExitStack() as pd:
            H2T = sb("H2T", [128, 8, TG * 128], BF16, pd)
            GT = [sb("GT%d" % i, [128, 128, 128], BF16, pd) for i in range(TG)]
            ccur, acur = [0], [0]
            ev = 0
            for g in range(NG):
                g0 = g * TG * 128
                nm = lambda s_: "%s_g%d" % (s_, g)
                T.dma("sp", out=H2T[:], in_=H2Ts[:, :, g0:g0 + TG * 128],
                      _r=["s_h2t_%d" % (g * TG + i) for i in range(TG)])
                with ExitStack() as cs:
                    RT = sb(nm("RT"), [128, 128, 128], BF16, cs)
                    WIT = sb(nm("WIT"), [128, 128, 128], BF16, cs)
                    SCt = sb(nm("SCd"), [128, 16, 128], F32, cs)
                    SMt = sb(nm("SMd"), [128, 8, 33], F32, cs)
                    kexp = sb(nm("kexp"), [128, 128], F32, cs)
                    kT = sb(nm("kT"), [128, 128], F32, cs)
                    Xb2 = [sb(nm("Xb%d" % i), [128, 8, 16, 16], F32, cs) for i in range(2)]
                    EXb2 = [sb(nm("EXb%d" % i), [128, 8, 16, 16], F32, cs) for i in range(2)]
                    Rb = [sb(nm("Rb%d" % i), [128, 8, 16, 16], BF16, cs) for i in range(2)]
                    WIb = [sb(nm("WIb%d" % i), [128, 8, 16, 16], BF16, cs) for i in range(2)]
                    for tl in range(TG):
                        ti = g * TG + tl
                        T.dma("sp", out=SCt[:], in_=SCs[ti], _r=["s_sc_%d" % ti])
                        T.dma("sp", out=SMt[:], in_=SMs[ti], _r=["s_sm_%d" % ti])
                        SC4 = SCt[:].rearrange("p (h k) n -> p h k n", k=2)
                        T.op("pool", "tensor_copy", out=kexp[:].rearrange("p (h a) -> p h a", h=8),
                             in_=SMt[:, :, 32:33].to_broadcast([128, 8, 16]))
                        p = psum_from(CP, ccur)
                        T.op("pe", "transpose", out=p[:, 0:128], in_=kexp[:], identity=identf[:])
                        T.pe_signal()
                        T.op("act", "copy", out=kT[:], in_=p[:, 0:128])
                        for jb in range(8):
                            js = slice(jb * 16, (jb + 1) * 16)
                            rb, wb = Rb[jb % 2], WIb[jb % 2]
                            Xb, EXb = Xb2[jb % 2], EXb2[jb % 2]
                            T.op("pool", "tensor_tensor", out=Xb[:],
                                 in0=SC4[:, :, 1, js].unsqueeze(2).to_broadcast([128, 8, 16, 16]),
                                 in1=SMt[:, :, 0:16].unsqueeze(3).to_broadcast([128, 8, 16, 16]), op=ALU.subtract)
                            T.op("act", "activation", out=EXb[:], in_=Xb[:], func=AF.Exp)
                            T.op("dve", "scalar_tensor_tensor", out=rb[:], in0=Xb[:], scalar=-1e-5, in1=EXb[:],
                                 op0=ALU.is_ge, op1=ALU.mult)
                            T.op("dve", "tensor_tensor", out=wb[:],
                                 in0=SC4[:, :, 0, js].unsqueeze(2).to_broadcast([128, 8, 16, 16]),
                                 in1=SMt[:, :, 16:32].unsqueeze(3).to_broadcast([128, 8, 16, 16]), op=ALU.is_equal)
                            rb2 = rb[:].rearrange("p h a j -> p (h a) j")
                            wb2 = wb[:].rearrange("p h a j -> p (h a) j")
                            for b in range(2):
                                c0 = jb * 16 + b * 8
                                p = psum_from(CP, ccur)
                                pbf = p[:].bitcast(BF16)
                                for q in range(8):
                                    T.op("pe", "transpose", out=pbf[:, q * 128:(q + 1) * 128], in_=rb2[:, :, b * 8 + q],
                                         identity=identb[:])
                                T.pe_signal()
                                T.op("dve", "tensor_tensor", out=RT[:, c0:c0 + 8, :],
                                     in0=pbf[:, 0:1024].rearrange("p (j t) -> p j t", j=8),
                                     in1=kT[:].unsqueeze(1).to_broadcast([128, 8, 128]), op=ALU.mult)
                                p = psum_from(CP, ccur)
                                pbf = p[:].bitcast(BF16)
                                for q in range(8):
                                    T.op("pe", "transpose", out=pbf[:, q * 128:(q + 1) * 128], in_=wb2[:, :, b * 8 + q],
                                         identity=identb[:])
                                T.pe_signal()
                                T.op("act", "copy", out=WIT[:, c0:c0 + 8, :],
                                     in_=pbf[:, 0:1024].rearrange("p (j t) -> p j t", j=8))
                        for t4 in range(32):
                            p = psum_from(CP, ccur)
                            for q in range(4):
                                t = t4 * 4 + q
                                T.op("pe", "matmul", out=p[:, q * 128:(q + 1) * 128], lhsT=RT[:, :, t], rhs=WIT[:, :, t],
                                     start=True, stop=True)
                            T.pe_signal()
                            dst = GT[tl][:, t4 * 4:(t4 + 1) * 4, :].rearrange("p t i -> p (t i)")
                            if ev % 2 == 0:
                                T.op("act", "copy", out=dst, in_=p[:, 0:512])
                            else:
                                T.op("dve", "tensor_copy", out=dst, in_=p[:, 0:512])
                            ev += 1
                T.barrier()
                with ExitStack() as es:
                    UTc = [sb(nm("UTc%d" % i), [128, 8, IC * 128], BF16, es) for i in range(2)]
                    Vc = [sb(nm("Vc%d" % i), [128, IC, D], BF16, es) for i in range(2)]
                    gA = [sb(nm("gA%d" % i), [128, IC, TG * 128], BF16, es) for i in range(2)]
                    GAm = [sb(nm("GAm%d" % i), [128, IC, 128], BF16, es) for i in range(4)]
                    h2f = sb(nm("h2f"), [128, D], F32, es)
                    rD = sb(nm("rD"), [128, D], F32, es)
                    tmpD = sb(nm("tmpD"), [128, D], F32, es)
                    stD = sb(nm("stD"), [128, 16], F32, es)
                    yo = [sb(nm("yo%d" % i), [128, D], F32, es) for i in range(2)]

                    def load_tables(ic):
                        T.dma("sp", out=UTc[ic % 2][:], in_=ut_v[:, :, ic * IC * 128:(ic + 1) * IC * 128], _r=["s_ut%d" % ic])
                        T.dma("sp", out=Vc[ic % 2][:], in_=vb_v2[ic], _r=["s_vb%d" % ic])

                    def emit_at(ic):
                        utc, ga = UTc[ic % 2], gA[ic % 2]
                        for c in range(IC):
                            p = psum_from(ATP, acur)
                            for k in range(8):
                                T.op("pe", "matmul", out=p[:, 0:TG * 128], lhsT=utc[:, k, c * 128:(c + 1) * 128],
                                     rhs=H2T[:, k, :], start=(k == 0), stop=(k == 7))
                            T.pe_signal()
                            T.op("act", "activation", out=ga[:, c, :], in_=p[:, 0:TG * 128], func=AF.Gelu)

                    def emit_gv(ic):
                        for tl in range(TG):
                            gam = GAm[(ic * TG + tl) % 4]
                            T.op("dve", "tensor_tensor", out=gam[:],
                                 in0=GT[tl][:, :, ic * IC:(ic + 1) * IC].rearrange("p t c -> p c t"),
                                 in1=gA[ic % 2][:, :, tl * 128:(tl + 1) * 128], op=ALU.mult)
                            for half in range(2):
                                po = PS[OPB[tl * 2 + half]]
                                for c in range(IC):
                                    T.op("pe", "matmul", out=po[:, 0:512], lhsT=gam[:, c, :],
                                         rhs=Vc[ic % 2][:, c, half * 512:(half + 1) * 512],
                                         start=(ic == 0 and c == 0), stop=(ic == NIC - 1 and c == IC - 1))
                            T.pe_signal()

                    load_tables(0)
                    emit_at(0)
                    for ic in range(NIC):
                        if ic + 1 < NIC:
                            load_tables(ic + 1)
                            emit_at(ic + 1)
                        emit_gv(ic)
                    for tl in range(TG):
                        ti = g * TG + tl
                        seq, tt = ti // NT, ti % NT
                        T.dma("sp", out=h2f[:], in_=H2s[ti * 128:(ti + 1) * 128, :], _r=["s_h2_%d" % ti])
                        for half in range(2):
                            T.op("dve", "scalar_tensor_tensor", out=rD[:, half * 512:(half + 1) * 512],
                                 in0=h2f[:, half * 512:(half + 1) * 512], scalar=ALPHA,
                                 in1=PS[OPB[tl * 2 + half]][:, 0:512], op0=ALU.mult, op1=ALU.add)
                        yo_ = yo[ti % 2]
                        layer_norm(rD, g3b, b3b, yo_[:], tmpD, stD)
                        T.dma("sp", out=y_out[seq, tt * 128:(tt + 1) * 128, :], in_=yo_[:])
                T.barrier()
        T.finish("sp", ["y_out"])
    return nc


def _prep_inputs(inputs, NSEQ, S, n_cores):
    sq = lambda a: np.ascontiguousarray(np.asarray(a)[0])
    shared = {
        "ln_in_g": np.asarray(inputs["ln_in_g"]), "ln_in_b": np.asarray(inputs["ln_in_b"]),
        "w_in": sq(inputs["w_in"]), "conv_w": sq(inputs["conv_w"]), "conv_b": sq(inputs["conv_b"]),
        "dt_bias": sq(inputs["dt_bias"]), "a_log": sq(inputs["a_log"]), "d_skip": sq(inputs["d_skip"]),
        "ssm_norm_g": sq(inputs["ssm_norm_g"]), "q_norm_g": sq(inputs["q_norm_g"]), "w_q_b": sq(inputs["w_q_b"]),
        "kv_norm_g": sq(inputs["kv_norm_g"]), "w_kv_b": sq(inputs["w_kv_b"]), "w_out": sq(inputs["w_out"]),
        "ln_mix_g": sq(inputs["ln_mix_g"]), "ln_mix_b": sq(inputs["ln_mix_b"]), "w_query": sq(inputs["w_query"]),
        "sub_keys": sq(inputs["sub_keys"]).reshape(16, 128, 128),
        "u_table": sq(inputs["u_table"]), "v_table": sq(inputs["v_table"]),
        "ln_ffn_g": sq(inputs["ln_ffn_g"]), "ln_ffn_b": sq(inputs["ln_ffn_b"]),
    }
    shared = {k: np.ascontiguousarray(v, dtype=np.float32) for k, v in shared.items()}
    shared.update(_consts(S))
    xs = np.asarray(inputs["x"], dtype=np.float32)
    maps = []
    for c in range(n_cores):
        m = dict(shared)
        m["x"] = np.ascontiguousarray(xs[c * NSEQ:(c + 1) * NSEQ])
        maps.append(m)
    return maps


def kernel(**inputs):
    B, S = inputs["x"].shape[0], inputs["x"].shape[1]
    n_cores = 8
    NSEQ = B // n_cores
    nc = build(NSEQ, S)
    maps = _prep_inputs(inputs, NSEQ, S, n_cores)
    res = run_bass_kernel_spmd(nc, maps, core_ids=list(range(n_cores)))
    return np.concatenate([np.asarray(r["y_out"], dtype=np.float32) for r in res.results], axis=0)
```
